# Optimizing a Trainium2 kernel written in Bass

```python
import math
import jax, jax.numpy as jnp
from jax import lax
import numpy as np

D_MODEL = 2048
BATCH = 4
SEQ = 8192
DEPTH = 1

ATTN_WIDTH = D_MODEL // 2
HEAD_DIM = 64
N_DIFF_HEADS = ATTN_WIDTH // (2 * HEAD_DIM)
SSM_WIDTH = D_MODEL // 2
SSM_GROUP = 16
N_SSM_GROUPS = SSM_WIDTH // SSM_GROUP
SSM_STATE = 64
D_FF = -(-8 * D_MODEL // (3 * 256)) * 256
IN_COLS = 3 * ATTN_WIDTH + SSM_WIDTH + 2 * D_MODEL
Q_BLOCK = 128
EPS = 1e-6
DT_MIN = 1e-3
DT_MAX = 1e-1
NEG_INF = -1e30

kernel_name = "hybrid_diffattn_s5_gated_block"


def _rms_norm(x, w):
    xf = x.astype(jnp.float32)
    y = xf * lax.rsqrt(jnp.mean(xf * xf, axis=-1, keepdims=True) + EPS)
    return (y * w.astype(jnp.float32)).astype(x.dtype)


def _lambda_init(layer_idx):
    return 0.8 - 0.6 * math.exp(-0.3 * layer_idx)


def _diff_attention(q, k, v, lam, lam_init, subln_w):
    bsz, seq, n_heads, _, dh = q.shape
    e = v.shape[-1]
    scale = dh ** -0.5
    n_blocks = seq // Q_BLOCK
    kpos = jnp.arange(seq)

    def one_block(i):
        start = i * Q_BLOCK
        qb = lax.dynamic_slice_in_dim(q, start, Q_BLOCK, axis=1)
        s = jnp.einsum('bqhcd,bkhcd->bhcqk', qb, k).astype(jnp.float32) * scale
        qpos = start + jnp.arange(Q_BLOCK)
        causal = kpos[None, :] <= qpos[:, None]
        s = jnp.where(causal, s, NEG_INF)
        p = jax.nn.softmax(s, axis=-1)
        a = p[:, :, 0] - lam * p[:, :, 1]
        return jnp.einsum('bhqk,bkhe->bqhe', a.astype(v.dtype), v)

    o = lax.map(one_block, jnp.arange(n_blocks))
    o = jnp.moveaxis(o, 0, 1).reshape(bsz, seq, n_heads, e)
    o = _rms_norm(o, subln_w) * (1.0 - lam_init)
    return o.reshape(bsz, seq, n_heads * e)


def _ssm_binop(left, right):
    a1, b1 = left
    a2, b2 = right
    return a1 * a2, a2 * b1 + b2


def _s5_groups(u, a_re, a_im, log_dt, b_re, b_im, c_re, c_im, d_skip):
    f32 = jnp.float32
    lam = lax.complex(a_re.astype(f32), a_im.astype(f32))
    dt = jnp.exp(log_dt.astype(f32))[:, None]
    a_bar = jnp.exp(lam * dt)
    b = lax.complex(b_re.astype(f32), b_im.astype(f32))
    b_bar = ((a_bar - 1.0) / lam)[..., None] * b
    c = lax.complex(c_re.astype(f32), c_im.astype(f32))
    d = d_skip.astype(f32)

    def one_sequence(u_seq):
        uf = u_seq.astype(f32)
        bu = jnp.einsum('gph,lgh->lgp', b_bar, uf.astype(jnp.complex64))
        a_seq = jnp.broadcast_to(a_bar, bu.shape)
        _, states = lax.associative_scan(_ssm_binop, (a_seq, bu), axis=0)
        y = jnp.einsum('ghp,lgp->lgh', c, states).real + d * uf
        return y.astype(u_seq.dtype)

    return lax.map(one_sequence, u)


def setup_inputs(seed: int = 0) -> dict:
    key = jax.random.key(seed)
    ks = jax.random.split(key, 32)
    f32 = jnp.float32

    def nrm(k, shape, scale):
        return jax.random.normal(k, shape, f32) * scale

    def gain(k, shape):
        return 1.0 + 0.01 * jax.random.normal(k, shape, f32)

    L_, G, P, H = DEPTH, N_SSM_GROUPS, SSM_STATE, SSM_GROUP
    x = jax.random.normal(ks[0], (BATCH, SEQ, D_MODEL), f32)
    a_re = -0.5 + 0.01 * jax.random.normal(ks[1], (L_, G, P), f32)
    a_im = math.pi * jnp.arange(P, dtype=f32)[None, None, :] + 0.01 * jax.random.normal(ks[2], (L_, G, P), f32)
    log_dt = jax.random.uniform(ks[3], (L_, G), f32, math.log(DT_MIN), math.log(DT_MAX))
    return {
        "x": x,
        "w_in": nrm(ks[4], (L_, D_MODEL, IN_COLS), D_MODEL ** -0.5),
        "lambda_q1": nrm(ks[5], (L_, HEAD_DIM), 0.1),
        "lambda_k1": nrm(ks[6], (L_, HEAD_DIM), 0.1),
        "lambda_q2": nrm(ks[7], (L_, HEAD_DIM), 0.1),
        "lambda_k2": nrm(ks[8], (L_, HEAD_DIM), 0.1),
        "subln_w": gain(ks[9], (L_, 2 * HEAD_DIM)),
        "ssm_a_re": a_re,
        "ssm_a_im": a_im,
        "ssm_log_dt": log_dt,
        "ssm_b_re": nrm(ks[10], (L_, G, P, H), (2 * H) ** -0.5),
        "ssm_b_im": nrm(ks[11], (L_, G, P, H), (2 * H) ** -0.5),
        "ssm_c_re": nrm(ks[12], (L_, G, H, P), P ** -0.5),
        "ssm_c_im": nrm(ks[13], (L_, G, H, P), P ** -0.5),
        "ssm_d": nrm(ks[14], (L_, G, H), 1.0),
        "w_glu": nrm(ks[15], (L_, SSM_WIDTH, SSM_WIDTH), SSM_WIDTH ** -0.5),
        "b_glu": nrm(ks[16], (L_, SSM_WIDTH), 0.01),
        "w_attn_branch": nrm(ks[17], (L_, ATTN_WIDTH, D_MODEL), ATTN_WIDTH ** -0.5),
        "w_ssm_branch": nrm(ks[18], (L_, SSM_WIDTH, D_MODEL), SSM_WIDTH ** -0.5),
        "w_out": nrm(ks[19], (L_, D_MODEL, D_MODEL), D_MODEL ** -0.5),
        "norm_mix_pre": gain(ks[20], (L_, D_MODEL)),
        "norm_mix_post": gain(ks[21], (L_, D_MODEL)),
        "w_ffn_gate": nrm(ks[22], (L_, D_MODEL, D_FF), D_MODEL ** -0.5),
        "w_ffn_up": nrm(ks[23], (L_, D_MODEL, D_FF), D_MODEL ** -0.5),
        "w_ffn_down": nrm(ks[24], (L_, D_FF, D_MODEL), D_FF ** -0.5),
        "norm_ffn_pre": gain(ks[25], (L_, D_MODEL)),
        "norm_ffn_post": gain(ks[26], (L_, D_MODEL)),
    }


def reference(x, w_in, lambda_q1, lambda_k1, lambda_q2, lambda_k2, subln_w,
              ssm_a_re, ssm_a_im, ssm_log_dt, ssm_b_re, ssm_b_im, ssm_c_re, ssm_c_im,
              ssm_d, w_glu, b_glu, w_attn_branch, w_ssm_branch, w_out,
              norm_mix_pre, norm_mix_post, w_ffn_gate, w_ffn_up, w_ffn_down,
              norm_ffn_pre, norm_ffn_post):
    bsz, seq, _ = x.shape
    splits = np.cumsum([ATTN_WIDTH, ATTN_WIDTH, ATTN_WIDTH, SSM_WIDTH, D_MODEL]).tolist()
    h = x
    for l in range(DEPTH):
        lam_init = _lambda_init(l)
        u = _rms_norm(h, norm_mix_pre[l])
        proj = u @ w_in[l]
        q, k, v, s_in, g_a, g_s = jnp.split(proj, splits, axis=-1)
        q = q.reshape(bsz, seq, N_DIFF_HEADS, 2, HEAD_DIM)
        k = k.reshape(bsz, seq, N_DIFF_HEADS, 2, HEAD_DIM)
        v = v.reshape(bsz, seq, N_DIFF_HEADS, 2 * HEAD_DIM)
        f32 = jnp.float32
        lam = (jnp.exp(jnp.sum(lambda_q1[l].astype(f32) * lambda_k1[l].astype(f32)))
               - jnp.exp(jnp.sum(lambda_q2[l].astype(f32) * lambda_k2[l].astype(f32)))
               + lam_init)
        y_a = _diff_attention(q, k, v, lam, lam_init, subln_w[l])

        s_u = s_in.reshape(bsz, seq, N_SSM_GROUPS, SSM_GROUP)
        y_s = _s5_groups(s_u, ssm_a_re[l], ssm_a_im[l], ssm_log_dt[l], ssm_b_re[l],
                         ssm_b_im[l], ssm_c_re[l], ssm_c_im[l], ssm_d[l])
        y_s = jax.nn.gelu(y_s.reshape(bsz, seq, SSM_WIDTH))
        y_s = y_s * jax.nn.sigmoid(y_s @ w_glu[l] + b_glu[l])

        merged = (jax.nn.sigmoid(g_a) * (y_a @ w_attn_branch[l])
                  + jax.nn.sigmoid(g_s) * (y_s @ w_ssm_branch[l]))
        h = h + _rms_norm(merged @ w_out[l], norm_mix_post[l])
        z = _rms_norm(h, norm_ffn_pre[l])
        f = (jax.nn.silu(z @ w_ffn_gate[l]) * (z @ w_ffn_up[l])) @ w_ffn_down[l]
        h = h + _rms_norm(f, norm_ffn_post[l])
    return h
```

```python
import math, contextlib
import numpy as np
import ml_dtypes
import concourse.bass as bass
import concourse.mybir as mybir
from concourse.bass_utils import run_bass_kernel_spmd

F32, BF16, I32 = mybir.dt.float32, mybir.dt.bfloat16, mybir.dt.int32
AF = mybir.ActivationFunctionType
ALU = mybir.AluOpType
AX = mybir.AxisListType

D = 2048; DC = 16; NT = 512; DFF = 5632; EPS = 1e-6
LAM_INIT = 0.8 - 0.6 * math.exp(0.0)
TWO_PI = 2.0 * math.pi
SIN_SCALE = TWO_PI * (1.0 - 4e-7)


class Res:
    def __init__(s, name, off=None, nbytes=0):
        s.name = name; s.w = None; s.rd = {}; s.al = []; s.dsem = None; s.dcnt = 0
        s.off = off; s.end = None if off is None else off + nbytes


class Eng:
    def __init__(s, name, sem):
        s.name = name; s.sem = sem; s.cnt = 0; s.waited = {}; s.prog = []


class Sched:
    def __init__(s, nc, stack):
        s.nc = nc; s.stack = stack; s.nsem = 0
        s.E = {n: Eng(n, s.newsem("e_" + n)) for n in ("pe", "act", "dve", "pool", "sp")}
        s.sb = []

    def newsem(s, name):
        s.nsem += 1
        return s.stack.enter_context(s.nc.semaphore(name))

    def res(s, name):
        return Res(name)

    def tile(s, name, shape, dt, off):
        esz = 4 if dt in (F32, I32) else 2
        nb = esz * int(np.prod(shape[1:]))
        t = s.nc.alloc_sbuf_tensor_at(name, list(shape), dt, offset=off)
        r = Res(name, off, nb)
        for o in s.sb:
            if o.off < r.end and r.off < o.end:
                o.al.append(r); r.al.append(o)
        s.sb.append(r)
        return t, r

    def _need(s, E, reads, writes):
        need = {}

        def add(sem, val, war):
            if sem is E.sem:
                if E.name == "pe" or war:
                    return
            if E.waited.get(sem, 0) >= val:
                return
            if need.get(sem, 0) < val:
                need[sem] = val

        for r in reads:
            if r.w: add(r.w[0], r.w[1], False)
        for w in writes:
            for x in [w] + w.al:
                if x.w: add(x.w[0], x.w[1], False)
                for sm, v in x.rd.items(): add(sm, v, True)
        for sm, v in need.items():
            E.waited[sm] = v
        return list(need.items())

    def _rec(s, reads, writes, tok):
        for r in reads:
            if r.rd.get(tok[0], 0) < tok[1]: r.rd[tok[0]] = tok[1]
        for w in writes:
            w.w = tok; w.rd = {}

    def op(s, eng, fn, reads=(), writes=()):
        E = s.E[eng]
        waits = s._need(E, reads, writes)
        E.cnt += 1
        s._rec(reads, writes, (E.sem, E.cnt))
        E.prog.append((waits, fn, (E.sem, 1)))

    def dma(s, pairs, reads, writes, semres, q="sp", **kw):
        E = s.E[q]
        waits = s._need(E, reads, writes)
        if semres.dsem is None:
            semres.dsem = s.newsem("d_" + semres.name)
        sem = semres.dsem
        semres.dcnt += 16 * len(pairs)
        s._rec(reads, writes, (sem, semres.dcnt))

        def fn(h, pairs=pairs, sem=sem, kw=kw):
            for o, i in pairs:
                h.dma_start(out=o, in_=i, **kw).then_inc(sem, 16)
            return None
        E.prog.append((waits, fn, None))

    def final_wait(s, q, res_list):
        E = s.E[q]
        waits = s._need(E, res_list, [])
        E.prog.append((waits, None, None))

    def replay(s, h, name):
        for waits, fn, inc in s.E[name].prog:
            for sem, val in waits:
                h.wait_ge(sem, val)
            if fn is None:
                continue
            ins = fn(h)
            if inc is not None:
                ins.then_inc(inc[0], inc[1])


def build_nc(NCH):
    SEQ = NCH * NT; NS = NCH // 2; NOWN = NS * NT
    nc = bass.Bass("TRN2", target_bir_lowering=False)
    stack = contextlib.ExitStack()
    S = Sched(nc, stack)

    def din(name, shape, dt=F32):
        return nc.dram_tensor(name, list(shape), dt, kind="ExternalInput").ap()

    def dscr(name, shape, dt):
        return nc.dram_tensor(name, list(shape), dt, kind="Internal").ap()

    xT_all = din("xT_all", [D, SEQ]); xT_own = din("xT_own", [D, NOWN])
    outT = nc.dram_tensor("outT", [D, NOWN], F32, kind="ExternalOutput").ap()
    w_in = din("w_in", [D, 8192]); w_glu = din("w_glu", [1024, 1024])
    w_a = din("w_a", [1024, D]); w_s = din("w_s", [1024, D]); w_o = din("w_o", [D, D])
    w_g = din("w_g", [D, DFF]); w_u = din("w_u", [D, DFF]); w_d = din("w_d", [DFF, D])
    vec16 = din("vec16", [128, 4, 16])
    bglu_d = din("bglu", [128, 8]); dvec_d = din("dvec", [128, 8])
    lam_d = din("lamv", [128, 4, 64]); subw_d = din("subw", [128, 1]); msel_d = din("msel", [128, 2])
    mask_d = din("mask", [128, 2, 4, 512], BF16)
    ident_d = din("ident", [128, 128], BF16)
    iota_d = din("iota", [128, 513])
    ssmA_d = din("ssmA", [128, 3, 32])
    ssmP_d = din("ssmP", [5, 128, 32, 128])
    ssmC_d = din("ssmC", [2, 128, 32, 128])

    def wscr(name, din_, dout):
        return dscr(name, [dout // 512, 128, din_ // 128, 512], BF16)
    W = {"in": (w_in, D, 8192), "glu": (w_glu, 1024, 1024), "a": (w_a, 1024, D), "s": (w_s, 1024, D),
         "o": (w_o, D, D), "g": (w_g, D, DFF), "u": (w_u, D, DFF), "d": (w_d, DFF, D)}
    WS = {k: wscr("ws_" + k, v[1], v[2]) for k, v in W.items()}
    WR = {k: [[S.res("wr_%s_%d_%d" % (k, fb, pc)) for pc in range((v[1] // 128 + 7) // 8)] for fb in range(v[2] // 512)]
          for k, v in W.items()}
    kscr = dscr("kscr", [8, 128, SEQ], BF16); vscr = dscr("vscr", [8, NCH, 128, 4, 128], BF16)
    kres = [S.res("kscr%d" % i) for i in range(NCH)]; vres = [S.res("vscr%d" % i) for i in range(NCH)]
    tab_s = dscr("tab_s", [32, 128, 2, 513], F32); bt_s = dscr("bt_s", [32, 128, 2, 128], BF16)
    ct_s = dscr("ct_s", [32, 128, 3, 128], BF16)
    tabres = [S.res("tabs%d" % q) for q in range(32)]; btres = [S.res("bts%d" % q) for q in range(32)]; ctres = [S.res("cts%d" % q) for q in range(32)]

    base = 16512
    off = [base]

    def P(name, shape, dt):
        esz = 4 if dt in (F32, I32) else 2
        nb = esz * int(np.prod(shape[1:])); nb = (nb + 31) // 32 * 32
        t = S.tile(name, shape, dt, off[0]); off[0] += nb
        return t
    WSL = [P("wsl%d" % i, [128, 16, 512], BF16) for i in range(3)]
    MASK = P("mask", [128, 2, 4, 512], BF16); IDENT = P("ident", [128, 128], BF16); ONES = P("ones", [128, 128], BF16)
    VEC16 = P("vec16", [128, 4, 16], F32); BGLU = P("bglu", [128, 8], F32); DVEC = P("dvec", [128, 8], F32)
    LAMV = P("lamv", [128, 4, 64], F32); SUBW = P("subw", [128, 1], F32); MSEL = P("msel", [128, 2], F32)
    NEGLAM = P("neglam", [128, 1], F32); LTMP = P("ltmp", [128, 4], F32)
    RA = P("ra", [128, 32], F32); TURNA = P("turna", [128, 32], F32); CS512 = P("cs512", [128, 32, 3], F32)
    CAR = P("car", [128, 32, 2], F32); SSMA = P("ssma", [128, 3, 32], F32); CTMP = P("ctmp", [128, 4], F32)
    YSO = P("yso", [128, 8, 512], F32)
    XB = off[0]
    assert XB % 32 == 0

    def X(name, shape, dt, o):
        return S.tile(name, shape, dt, XB + o)
    XR = [X("xr%d" % i, [128, 512], F32, i * 2048) for i in range(3)]
    SQ = [X("sq%d" % i, [128, 512], BF16, 6144 + i * 1024) for i in range(2)]
    RSTD = X("rstd", [128, 512], F32, 8192)
    UA = X("ua", [128, 16, 512], BF16, 10240)
    KTEV = X("ktev", [128, 8, 512], BF16, 26624); VEV = X("vev", [128, 4, 1024], BF16, 34816)
    SIN = [X("sin%d" % i, [128, 8, 512], BF16, 61440 + i * 8192) for i in range(2)]
    T4 = [X("t%d" % i, [128, 512], F32, 77824 + i * 2048) for i in range(4)]
    WW = [[X("w%d_%d" % (b, j), [128, 513], F32, 86016 + (b * 2 + j) * 2080) for j in range(2)] for b in range(3)]
    XX = [[X("x%d_%d" % (b, j), [128, 512], BF16, 98496 + (b * 4 + j) * 1024) for j in range(4)] for b in range(3)]
    TABS = []
    for i in range(2):
        o = 110784 + i * 5408
        TABS.append((X("tab%d" % i, [128, 2, 513], F32, o), X("bt%d" % i, [128, 2, 128], BF16, o + 4128),
                     X("ct%d" % i, [128, 3, 128], BF16, o + 4640)))
    EPI1 = X("epi1", [128, 512], F32, 121600)
    OE = [X("oe%d" % i, [128, 512], F32, 123648 + i * 2048) for i in range(4)]
    U = X("u", [128, 16, 512], BF16, 0)
    QP = X("qp", [128, 8, 2, 512], BF16, 16384)
    KTW = [X("ktw%d" % i, [128, 512], BF16, 32768 + i * 1024) for i in range(4)]
    VTW = [X("vtw%d" % i, [128, 4, 128], BF16, 36864 + i * 1024) for i in range(4)]
    PT = [X("pt%d" % i, [128, 512], BF16, 40960 + i * 1024) for i in range(4)]
    EP = [X("ep%d" % i, [128, 512], F32, 45056 + i * 2048) for i in range(3)]
    SQB = X("sqb", [128, 512], BF16, 51200)
    YA = X("ya", [128, 8, 512], BF16, 52224); GB = X("gb", [128, 8, 512], BF16, 60416)
    YSB = X("ysb", [128, 8, 512], BF16, 68608)
    SA = X("sa", [128, 4, 512], F32, 76800); MT = X("mt", [128, 4, 512], F32, 84992)
    M = X("m", [128, 16, 512], BF16, 93184)
    BIG1 = X("big1", [128, 16, 512], F32, 16384)
    H = X("h", [128, 16, 512], F32, 49152)
    A = X("a", [128, 24, 512], BF16, 109568)
    RSTD2 = X("rstd2", [128, 512], F32, 84992); SQ2 = [X("sq2_%d" % i, [128, 512], BF16, 87040 + i * 1024) for i in range(2)]
    FT = [X("ft%d" % i, [128, 512], F32, 89088 + i * 2048) for i in range(2)]
    XR2 = [X("xr2_%d" % i, [128, 512], F32, 77824 + i * 2048) for i in range(3)]
    SQ2E = [X("sq2e_%d" % i, [128, 512], BF16, 83968 + i * 1024) for i in range(2)]; RSTD2E = X("rstd2e", [128, 512], F32, 86016)
    assert XB + 109568 + 24576 <= 229344
    ST = [X("st%d" % i, [128, 1024], F32, i * 4096) for i in range(16)]
    STI = X("sti", [128, 1024], I32, 16 * 4096)
    BTST = X("btst", [128, 8, 2, 128], BF16, 17 * 4096)
    CTST = X("ctst", [128, 8, 3, 128], BF16, 18 * 4096)
    TBST = [X("tbst%d" % i, [128, 2, 513], F32, 20 * 4096 + i * 4128) for i in range(2)]
    STG = [X("stg%d" % i, [128, 8, 512], F32, i * 16384) for i in range(4)]
    OBUF = [X("obuf%d" % i, [128, 8, 512], BF16, 94208 + i * 8192) for i in range(3)]
    PS = []
    for i in range(8):
        t = stack.enter_context(nc.psum_tensor("ps%d" % i, [128, 512], F32))
        PS.append((t, S.res("ps%d" % i)))

    def mm(ps, lhsT, rhs, start, stop, reads):
        S.op("pe", lambda h: h.matmul(ps[0][:], lhsT=lhsT, rhs=rhs, start=start, stop=stop),
             reads, [ps[1]])

    def act(out, in_, func, reads, writes, scale=None, bias=None):
        kw = {}
        if scale is not None: kw["scale"] = scale
        if bias is not None: kw["bias"] = bias
        S.op("act", lambda h: h.activation(out=out, in_=in_, func=func, **kw), reads, writes)

    def tt(eng, out, in0, in1, op, reads, writes):
        S.op(eng, lambda h: h.tensor_tensor(out=out, in0=in0, in1=in1, op=op), reads, writes)

    def ts(eng, out, in0, s1, op0, reads, writes, s2=None, op1=None):
        if op1 is None:
            S.op(eng, lambda h: h.tensor_scalar(out=out, in0=in0, scalar1=s1, scalar2=None, op0=op0), reads, writes)
        else:
            S.op(eng, lambda h: h.tensor_scalar(out=out, in0=in0, scalar1=s1, scalar2=s2, op0=op0, op1=op1), reads, writes)

    def stt(out, in0, sc, in1, op0, op1, reads, writes):
        S.op("dve", lambda h: h.scalar_tensor_tensor(out=out, in0=in0, scalar=sc, in1=in1, op0=op0, op1=op1), reads, writes)

    def recip(out, in_, reads, writes):
        S.op("dve", lambda h: h.reciprocal(out=out, in_=in_), reads, writes)

    def cp(eng, out, in_, reads, writes):
        S.op(eng, lambda h: h.tensor_copy(out=out, in_=in_), reads, writes)

    params = S.res("params")

    small = [(VEC16, vec16), (BGLU, bglu_d), (DVEC, dvec_d), (LAMV, lam_d), (SUBW, subw_d), (MSEL, msel_d),
             (MASK, mask_d), (IDENT, ident_d), (SSMA, ssmA_d)]
    S.dma([(t[0][:], d) for t, d in small], [], [t[1] for t, _ in small], params)
    S.op("pool", lambda h: h.memset(ONES[0][:], 1.0), [], [ONES[1]])
    S.op("pool", lambda h: h.memset(CAR[0][:], 0.0), [], [CAR[1]])

    cctr = [0]

    def cast_w(k, fbs):
        src, din_, dout = W[k]
        ndc = din_ // 128
        for fb in fbs:
            sv = src[:, fb * 512:(fb + 1) * 512].rearrange("(dc p) f -> p dc f", p=128)
            for pc in range((ndc + 7) // 8):
                d0 = pc * 8; n = min(8, ndc - d0)
                stg = STG[cctr[0] % 4]
                sl = (WSL + OBUF)[cctr[0] % 6]
                S.dma([(stg[0][:, 0:n, :], sv[:, d0:d0 + n, :])], [], [stg[1]], stg[1])
                e = cctr[0] % 4
                if e == 1:
                    act(sl[0][:, 0:n, :], stg[0][:, 0:n, :], AF.Copy, [stg[1]], [sl[1]])
                elif e == 3:
                    cp("dve", sl[0][:, 0:n, :], stg[0][:, 0:n, :], [stg[1]], [sl[1]])
                else:
                    cp("pool", sl[0][:, 0:n, :], stg[0][:, 0:n, :], [stg[1]], [sl[1]])
                S.dma([(WS[k][fb][:, d0:d0 + n, :], sl[0][:, 0:n, :])], [sl[1]], [WR[k][fb][pc]], sl[1])
                cctr[0] += 1
    cast_w("in", range(2, 8)); cast_w("in", [0, 1]); cast_w("glu", range(2)); cast_w("in", range(8, 16))
    cast_w("a", range(4)); cast_w("s", range(4)); cast_w("o", range(4))
    cast_w("g", range(11)); cast_w("u", range(11)); cast_w("d", range(4))

    L = LAMV[0]
    tt("dve", L[:, 0, :], L[:, 0, :], L[:, 1, :], ALU.mult, [LAMV[1]], [LAMV[1]])
    tt("dve", L[:, 2, :], L[:, 2, :], L[:, 3, :], ALU.mult, [LAMV[1]], [LAMV[1]])
    S.op("dve", lambda h: h.reduce_sum(out=LTMP[0][:, 0:1], in_=L[:, 0, :], axis=AX.X), [LAMV[1]], [LTMP[1]])
    S.op("dve", lambda h: h.reduce_sum(out=LTMP[0][:, 1:2], in_=L[:, 2, :], axis=AX.X), [LAMV[1]], [LTMP[1]])
    act(LTMP[0][:, 2:4], LTMP[0][:, 0:2], AF.Exp, [LTMP[1]], [LTMP[1]])
    tt("dve", NEGLAM[0][:], LTMP[0][:, 3:4], LTMP[0][:, 2:3], ALU.subtract, [LTMP[1]], [NEGLAM[1]])
    ts("dve", NEGLAM[0][:], NEGLAM[0][:], -LAM_INIT, ALU.add, [NEGLAM[1]], [NEGLAM[1]])
    ts("dve", SUBW[0][:], SUBW[0][:], 1.0 - LAM_INIT, ALU.mult, [SUBW[1]], [SUBW[1]])

    def sincos(t_ap, t_res, n, sin_ap, cos_ap, out_res, tmpf, tmpg, tmpi):
        fa, fr = tmpf[0][:, 0:n], tmpf[1]
        ga, gr = tmpg[0][:, 0:n], tmpg[1]
        ia, ir = tmpi[0][:, 0:n], tmpi[1]
        cp("dve", ia, t_ap, [t_res], [ir])
        cp("dve", ga, ia, [ir], [gr])
        tt("dve", fa, t_ap, ga, ALU.subtract, [t_res, gr], [fr])
        for rep in range(2):
            ts("dve", ga, fa, 0.5, ALU.is_gt, [fr], [gr])
            tt("dve", fa, fa, ga, ALU.subtract, [fr, gr], [fr])
            ts("dve", ga, fa, -0.5, ALU.is_lt, [fr], [gr])
            tt("dve", fa, fa, ga, ALU.add, [fr, gr], [fr])
            if rep == 0:
                act(sin_ap, fa, AF.Sin, [fr], [out_res], scale=SIN_SCALE)
                ts("dve", fa, fa, 0.25, ALU.add, [fr], [fr])
            else:
                act(cos_ap, fa, AF.Sin, [fr], [out_res], scale=SIN_SCALE)

    a = SSMA[0]
    act(a[:, 2, :], a[:, 2, :], AF.Exp, [SSMA[1]], [SSMA[1]])
    tt("dve", a[:, 0, :], a[:, 0, :], a[:, 2, :], ALU.mult, [SSMA[1]], [SSMA[1]])
    act(RA[0][:], a[:, 0, :], AF.Exp, [SSMA[1]], [RA[1]])
    tt("dve", a[:, 1, :], a[:, 1, :], a[:, 2, :], ALU.mult, [SSMA[1]], [SSMA[1]])
    ts("dve", TURNA[0][:], a[:, 1, :], 1.0 / TWO_PI, ALU.mult, [SSMA[1]], [TURNA[1]])

    IOTA = ST[15]
    S.dma([(IOTA[0][:, 0:513], iota_d)], [], [IOTA[1]], IOTA[1], q="act")
    for q in range(32):
        tb = TBST[q % 2]
        ts("dve", ST[12][0][:, 0:513], IOTA[0][:, 0:513], TURNA[0][:, q:q + 1], ALU.mult, [IOTA[1], TURNA[1]], [ST[12][1]])
        sincos(ST[12][0][:, 0:513], ST[12][1], 513, tb[0][:, 1, :], tb[0][:, 0, :], tb[1], ST[13], ST[14], STI)
        cp("dve", CS512[0][:, q, 0:1], tb[0][:, 0, 512:513], [tb[1]], [CS512[1]])
        cp("dve", CS512[0][:, q, 1:2], tb[0][:, 1, 512:513], [tb[1]], [CS512[1]])
        ts("dve", CS512[0][:, q, 2:3], tb[0][:, 1, 512:513], -1.0, ALU.mult, [tb[1]], [CS512[1]])
        S.dma([(tab_s[q], tb[0][:])], [tb[1]], [tabres[q]], tb[1], q="act")

    for qg in range(4):
        qs = slice(qg * 8, qg * 8 + 8)
        lds = []
        for j in range(5):
            lds.append((ST[j][0][:].rearrange("p (q c) -> p q c", q=8), ssmP_d[j, :, qs, :]))
        S.dma(lds, [], [ST[j][1] for j in range(5)], ST[0][1], q="act")
        are, aim, ldt, bre, bim = [ST[j] for j in range(5)]
        f = lambda t: t[0][:]
        act(f(ldt), f(ldt), AF.Exp, [ldt[1]], [ldt[1]])
        tt("dve", f(ST[5]), f(are), f(ldt), ALU.mult, [are[1], ldt[1]], [ST[5][1]])
        act(f(ST[5]), f(ST[5]), AF.Exp, [ST[5][1]], [ST[5][1]])
        tt("dve", f(ST[6]), f(aim), f(ldt), ALU.mult, [aim[1], ldt[1]], [ST[6][1]])
        ts("dve", f(ST[6]), f(ST[6]), 1.0 / TWO_PI, ALU.mult, [ST[6][1]], [ST[6][1]])
        sincos(f(ST[6]), ST[6][1], 1024, f(ST[7]), f(ST[8]), ST[7][1], ST[9], ST[10], STI)
        tt("dve", f(ST[8]), f(ST[8]), f(ST[5]), ALU.mult, [ST[7][1], ST[5][1]], [ST[8][1]])
        ts("dve", f(ST[8]), f(ST[8]), -1.0, ALU.add, [ST[8][1]], [ST[8][1]])
        tt("dve", f(ST[7]), f(ST[7]), f(ST[5]), ALU.mult, [ST[7][1], ST[5][1]], [ST[7][1]])
        tt("dve", f(ST[9]), f(are), f(are), ALU.mult, [are[1]], [ST[9][1]])
        tt("dve", f(ST[10]), f(aim), f(aim), ALU.mult, [aim[1]], [ST[10][1]])
        tt("dve", f(ST[9]), f(ST[9]), f(ST[10]), ALU.add, [ST[9][1], ST[10][1]], [ST[9][1]])
        recip(f(ST[9]), f(ST[9]), [ST[9][1]], [ST[9][1]])
        tt("dve", f(ST[10]), f(ST[8]), f(are), ALU.mult, [ST[8][1], are[1]], [ST[10][1]])
        tt("dve", f(ST[11]), f(ST[7]), f(aim), ALU.mult, [ST[7][1], aim[1]], [ST[11][1]])
        tt("dve", f(ST[10]), f(ST[10]), f(ST[11]), ALU.add, [ST[10][1], ST[11][1]], [ST[10][1]])
        tt("dve", f(ST[10]), f(ST[10]), f(ST[9]), ALU.mult, [ST[10][1], ST[9][1]], [ST[10][1]])
        tt("dve", f(ST[11]), f(ST[7]), f(are), ALU.mult, [ST[7][1], are[1]], [ST[11][1]])
        tt("dve", f(ST[12]), f(ST[8]), f(aim), ALU.mult, [ST[8][1], aim[1]], [ST[12][1]])
        tt("dve", f(ST[11]), f(ST[11]), f(ST[12]), ALU.subtract, [ST[11][1], ST[12][1]], [ST[11][1]])
        tt("dve", f(ST[11]), f(ST[11]), f(ST[9]), ALU.mult, [ST[11][1], ST[9][1]], [ST[11][1]])
        tt("dve", f(ST[12]), f(ST[10]), f(bre), ALU.mult, [ST[10][1], bre[1]], [ST[12][1]])
        tt("dve", f(ST[13]), f(ST[11]), f(bim), ALU.mult, [ST[11][1], bim[1]], [ST[13][1]])
        tt("dve", BTST[0][:, :, 0, :], ST[12][0][:].rearrange("p (q c) -> p q c", q=8), ST[13][0][:].rearrange("p (q c) -> p q c", q=8),
           ALU.subtract, [ST[12][1], ST[13][1]], [BTST[1]])
        tt("dve", f(ST[12]), f(ST[10]), f(bim), ALU.mult, [ST[10][1], bim[1]], [ST[12][1]])
        tt("dve", f(ST[13]), f(ST[11]), f(bre), ALU.mult, [ST[11][1], bre[1]], [ST[13][1]])
        tt("dve", BTST[0][:, :, 1, :], ST[12][0][:].rearrange("p (q c) -> p q c", q=8), ST[13][0][:].rearrange("p (q c) -> p q c", q=8),
           ALU.add, [ST[12][1], ST[13][1]], [BTST[1]])
        S.dma([(bt_s[qs].rearrange("q p r c -> p q r c"), BTST[0][:])], [BTST[1]], [btres[q] for q in range(qg * 8, qg * 8 + 8)], BTST[1], q="act")
        S.dma([(ST[0][0][:].rearrange("p (q c) -> p q c", q=8), ssmC_d[0, :, qs, :]),
               (ST[1][0][:].rearrange("p (q c) -> p q c", q=8), ssmC_d[1, :, qs, :])], [], [ST[0][1], ST[1][1]], ST[0][1], q="act")
        cp("dve", CTST[0][:, :, 0, :], ST[0][0][:].rearrange("p (q c) -> p q c", q=8), [ST[0][1]], [CTST[1]])
        ts("dve", CTST[0][:, :, 1, :], ST[0][0][:].rearrange("p (q c) -> p q c", q=8), -1.0, ALU.mult, [ST[0][1]], [CTST[1]])
        ts("dve", CTST[0][:, :, 2, :], ST[1][0][:].rearrange("p (q c) -> p q c", q=8), -1.0, ALU.mult, [ST[1][1]], [CTST[1]])
        S.dma([(ct_s[qs].rearrange("q p r c -> p q r c"), CTST[0][:])], [CTST[1]], [ctres[q] for q in range(qg * 8, qg * 8 + 8)], CTST[1], q="act")

    wctr = [0]

    def load_w(k, fb, dc0=0, ndc=None):
        src, din_, dout = W[k]
        if ndc is None: ndc = din_ // 128
        sl = WSL[wctr[0] % 3]; wctr[0] += 1
        S.dma([(sl[0][:, 0:ndc, :], WS[k][fb][:, dc0:dc0 + ndc, :])], list(WR[k][fb]), [sl[1]], sl[1])
        return sl

    psctr = [0]

    def rot(banks):
        b = banks[psctr[0] % len(banks)]; psctr[0] += 1
        return PS[b]

    def rms_rstd(pieces, rstd, sqs, bank):
        n = len(pieces)
        for i, (ap, r) in enumerate(pieces):
            sq = sqs[i % 2]
            act(sq[0][:], ap, AF.Square, [r], [sq[1]])
            mm(bank, ONES[0][:], sq[0][:], i == 0, i == n - 1, [ONES[1], sq[1]])
        act(rstd[0][:], bank[0][:], AF.Sqrt, [bank[1]], [rstd[1]], scale=1.0 / D, bias=EPS)
        recip(rstd[0][:], rstd[0][:], [rstd[1]], [rstd[1]])

    def norm_x(xsrc, col0, xr, sqs, rstd, gidx, dst, bank):
        pcs = []
        for dc in range(DC):
            r = xr[dc % 3]
            S.dma([(r[0][:], xsrc[dc * 128:(dc + 1) * 128, col0:col0 + NT])], [], [r[1]], r[1])
            sq = sqs[dc % 2]
            act(sq[0][:], r[0][:], AF.Square, [r[1]], [sq[1]])
            mm(bank, ONES[0][:], sq[0][:], dc == 0, dc == DC - 1, [ONES[1], sq[1]])
        act(rstd[0][:], bank[0][:], AF.Sqrt, [bank[1]], [rstd[1]], scale=1.0 / D, bias=EPS)
        recip(rstd[0][:], rstd[0][:], [rstd[1]], [rstd[1]])
        for dc in range(DC):
            r = xr[dc % 3]
            S.dma([(r[0][:], xsrc[dc * 128:(dc + 1) * 128, col0:col0 + NT])], [], [r[1]], r[1])
            stt(dst[0][:, dc, :], r[0][:], VEC16[0][:, gidx, dc:dc + 1], rstd[0][:], ALU.mult, ALU.mult,
                [r[1], VEC16[1], rstd[1]], [dst[1]])

    def proj_fm(k, fb, act_t, ndc, evac):
        wt = load_w(k, fb)
        for j in range(4):
            ps = rot([0, 1, 2, 3])
            for dc in range(ndc):
                mm(ps, wt[0][:, dc, j * 128:(j + 1) * 128], act_t[0][:, dc, :], dc == 0, dc == ndc - 1, [wt[1], act_t[1]])
            evac(fb * 4 + j, ps)

    for s in range(NS):
        for ci in range(2):
            i = 2 * s + ci
            norm_x(xT_all, i * NT, XR, SQ, RSTD, 0, UA, PS[0])
            for fb in (2, 3):
                proj_fm("in", fb, UA, DC, lambda fc, ps: act(KTEV[0][:, fc - 8, :], ps[0][:], AF.Copy, [ps[1]], [KTEV[1]]))
            S.dma([(kscr[:, :, i * NT:(i + 1) * NT].rearrange("h p t -> p h t"), KTEV[0][:])], [KTEV[1]], [kres[i]], KTEV[1])
            for fbv in (4, 5):
                wt = load_w("in", fbv)
                for t4 in range(4):
                    ps = rot([0, 1, 2, 3])
                    for dc in range(DC):
                        mm(ps, UA[0][:, dc, t4 * 128:(t4 + 1) * 128], wt[0][:, dc, :], dc == 0, dc == DC - 1, [wt[1], UA[1]])
                    cp("dve", VEV[0][:, t4, (fbv - 4) * 512:(fbv - 3) * 512], ps[0][:], [ps[1]], [VEV[1]])
            S.dma([(vscr[:, i].rearrange("h p t e -> p t h e"), VEV[0][:].rearrange("p t (h e) -> p t h e", h=8))],
                  [VEV[1]], [vres[i]], VEV[1])
            for fb in (6, 7):
                proj_fm("in", fb, UA, DC, lambda fc, ps, ci=ci: act(SIN[ci][0][:, fc - 24, :], ps[0][:], AF.Copy, [ps[1]], [SIN[ci][1]]))
        def ssm_ab(q, ci):
            o = q // 4; j = q * 2 + ci
            tb, bt, ct = TABS[q % 2]
            if ci == 0:
                S.dma([(tb[0][:], tab_s[q]), (bt[0][:], bt_s[q]), (ct[0][:], ct_s[q])], [tabres[q], btres[q], ctres[q]],
                      [tb[1], bt[1], ct[1]], tb[1])
            tres = [tb[1], bt[1], ct[1]]
            Zre, Zim = PS[4], PS[5]
            mm(Zre, bt[0][:, 0, :], SIN[ci][0][:, o, :], True, True, tres + [SIN[ci][1]])
            mm(Zim, bt[0][:, 1, :], SIN[ci][0][:, o, :], True, True, tres + [SIN[ci][1]])
            cosv, sinv = tb[0][:, 0, 1:513], tb[0][:, 1, 1:513]
            t1, t2, t3, t4 = T4
            tt("dve", t1[0][:], Zre[0][:], cosv, ALU.mult, [Zre[1]] + tres, [t1[1]])
            tt("dve", t2[0][:], Zim[0][:], sinv, ALU.mult, [Zim[1]] + tres, [t2[1]])
            tt("dve", t3[0][:], Zim[0][:], cosv, ALU.mult, [Zim[1]] + tres, [t3[1]])
            tt("dve", t4[0][:], Zre[0][:], sinv, ALU.mult, [Zre[1]] + tres, [t4[1]])
            tt("dve", t1[0][:], t1[0][:], t2[0][:], ALU.add, [t1[1], t2[1]], [t1[1]])
            tt("dve", t3[0][:], t3[0][:], t4[0][:], ALU.subtract, [t3[1], t4[1]], [t3[1]])
            wre, wim = WW[j % 3]
            rb = RA[0][:, q:q + 1].to_broadcast([128, 512])
            S.op("dve", lambda h: h.tensor_tensor_scan(out=wre[0][:, 1:513], data0=rb, data1=t1[0][:],
                 initial=CAR[0][:, q, 0:1], op0=ALU.mult, op1=ALU.add), [RA[1], t1[1], CAR[1]], [wre[1]])
            S.op("dve", lambda h: h.tensor_tensor_scan(out=wim[0][:, 1:513], data0=rb, data1=t3[0][:],
                 initial=CAR[0][:, q, 1:2], op0=ALU.mult, op1=ALU.add), [RA[1], t3[1], CAR[1]], [wim[1]])
            ts("dve", CTMP[0][:, 0:1], wre[0][:, 512:513], CS512[0][:, q, 0:1], ALU.mult, [wre[1], CS512[1]], [CTMP[1]])
            ts("dve", CTMP[0][:, 1:2], wre[0][:, 512:513], CS512[0][:, q, 1:2], ALU.mult, [wre[1], CS512[1]], [CTMP[1]])
            stt(CAR[0][:, q, 0:1], wim[0][:, 512:513], CS512[0][:, q, 2:3], CTMP[0][:, 0:1], ALU.mult, ALU.add,
                [wim[1], CS512[1], CTMP[1]], [CAR[1]])
            stt(CAR[0][:, q, 1:2], wim[0][:, 512:513], CS512[0][:, q, 0:1], CTMP[0][:, 1:2], ALU.mult, ALU.add,
                [wim[1], CS512[1], CTMP[1]], [CAR[1]])
            x1, x2, x3, x4 = XX[j % 3]
            tt("pool", x1[0][:], wre[0][:, 1:513], cosv, ALU.mult, [wre[1]] + tres, [x1[1]])
            tt("pool", x2[0][:], wim[0][:, 1:513], sinv, ALU.mult, [wim[1]] + tres, [x2[1]])
            tt("pool", x3[0][:], wre[0][:, 1:513], sinv, ALU.mult, [wre[1]] + tres, [x3[1]])
            tt("pool", x4[0][:], wim[0][:, 1:513], cosv, ALU.mult, [wim[1]] + tres, [x4[1]])

        def ssm_c(q, ci):
            o = q // 4; j = q * 2 + ci
            tb, bt, ct = TABS[q % 2]
            tres = [tb[1], bt[1], ct[1]]
            x1, x2, x3, x4 = XX[j % 3]
            Y = PS[6 + ci]
            first = (q % 4 == 0); last = (q % 4 == 3)
            mm(Y, ct[0][:, 0, :], x1[0][:], first, False, tres + [x1[1]])
            mm(Y, ct[0][:, 1, :], x2[0][:], False, False, tres + [x2[1]])
            mm(Y, ct[0][:, 2, :], x3[0][:], False, False, tres + [x3[1]])
            mm(Y, ct[0][:, 2, :], x4[0][:], False, last, tres + [x4[1]])
            if last:
                stt(EPI1[0][:], SIN[ci][0][:, o, :], DVEC[0][:, o:o + 1], Y[0][:], ALU.mult, ALU.add,
                    [SIN[ci][1], DVEC[1], Y[1]], [EPI1[1]])
                if ci == 0:
                    ts("dve", YSO[0][:, o, :], EPI1[0][:], MSEL[0][:, 0:1], ALU.mult, [EPI1[1], MSEL[1]], [YSO[1]])
                else:
                    stt(YSO[0][:, o, :], EPI1[0][:], MSEL[0][:, 1:2], YSO[0][:, o, :], ALU.mult, ALU.add,
                        [EPI1[1], MSEL[1], YSO[1]], [YSO[1]])
        ssm_i = [0]

        def ssm_step():
            j = ssm_i[0]
            if j < 64:
                ssm_ab(j // 2, j % 2)
            if 0 <= j - 2 < 64:
                ssm_c((j - 2) // 2, (j - 2) % 2)
            ssm_i[0] += 1

        c0 = s * NT
        norm_x(xT_own, c0, XR2, SQ2E, RSTD2E, 0, U, PS[0])
        S.op("pool", lambda h: h.memset(QP[0][64:128, :, 0, :], 0.0), [], [QP[1]])
        S.op("pool", lambda h: h.memset(QP[0][0:64, :, 1, :], 0.0), [], [QP[1]])

        def evq(fc, ps):
            act(QP[0][0:64, fc, 0, :], ps[0][0:64, :], AF.Copy, [ps[1]], [QP[1]])
            act(QP[0][64:128, fc, 1, :], ps[0][64:128, :], AF.Copy, [ps[1]], [QP[1]])
        for fb in (0, 1):
            proj_fm("in", fb, U, DC, evq)
        NK = 2 * s + 2
        kvctr = [0]; ptctr = [0]
        total_tiles = 8 * 2 * NK * 4; tile_ctr = [0]
        Ob, Zb = PS[2], PS[3]
        for hd in range(8):
            tiles = [(c, kc, kt) for c in range(2) for kc in range(NK) for kt in range(4)]
            loaded = {}
            sbank = {}

            def emit_qk(i):
                c, kc, kt = tiles[i]
                if (c, kc) not in loaded:
                    n_ = kvctr[0]; kvctr[0] += 1
                    kw_, vw_ = KTW[n_ % 4], VTW[n_ % 4]
                    S.dma([(kw_[0][:], kscr[hd, :, kc * NT:(kc + 1) * NT])], [kres[kc]], [kw_[1]], kw_[1])
                    S.dma([(vw_[0][:], vscr[hd, kc])], [vres[kc]], [vw_[1]], vw_[1])
                    loaded[(c, kc)] = (kw_, vw_)
                kw_, vw_ = loaded[(c, kc)]
                sp_ = rot([0, 1])
                diag = kc >= NK - 2
                mm(sp_, kw_[0][:, kt * 128:(kt + 1) * 128], QP[0][:, hd, c, :], True, not diag, [kw_[1], QP[1]])
                if diag:
                    mm(sp_, IDENT[0][:], MASK[0][:, kc - (NK - 2), kt, :], False, True, [IDENT[1], MASK[1]])
                sbank[i] = sp_

            emit_qk(0)
            for i in range(len(tiles)):
                if i + 1 < len(tiles):
                    emit_qk(i + 1)
                c, kc, kt = tiles[i]
                kw_, vw_ = loaded[(c, kc)]
                sp_ = sbank.pop(i)
                pt = PT[ptctr[0] % 4]; ptctr[0] += 1
                act(pt[0][:], sp_[0][:], AF.Exp, [sp_[1]], [pt[1]], scale=0.125)
                fst = (kc == 0 and kt == 0); lst = (kc == NK - 1 and kt == 3)
                mm(Ob, vw_[0][:, kt, :], pt[0][:], fst, lst, [vw_[1], pt[1]])
                mm(Zb, ONES[0][:], pt[0][:], fst, lst, [ONES[1], pt[1]])
                if lst:
                    act(OE[2 * c][0][:], Ob[0][:], AF.Copy, [Ob[1]], [OE[2 * c][1]])
                    act(OE[2 * c + 1][0][:], Zb[0][:], AF.Copy, [Zb[1]], [OE[2 * c + 1][1]])
                tile_ctr[0] += 1
                while ssm_i[0] < (tile_ctr[0] * 66) // total_tiles:
                    ssm_step()
            ea, eb, ec = EP
            recip(OE[1][0][:], OE[1][0][:], [OE[1][1]], [OE[1][1]])
            tt("dve", eb[0][:], OE[0][0][:], OE[1][0][:], ALU.mult, [OE[0][1], OE[1][1]], [eb[1]])
            recip(OE[3][0][:], OE[3][0][:], [OE[3][1]], [OE[3][1]])
            tt("dve", ec[0][:], OE[2][0][:], OE[3][0][:], ALU.mult, [OE[2][1], OE[3][1]], [ec[1]])
            stt(eb[0][:], ec[0][:], NEGLAM[0][:, 0:1], eb[0][:], ALU.mult, ALU.add, [ec[1], NEGLAM[1], eb[1]], [eb[1]])
            tt("dve", SQB[0][:], eb[0][:], eb[0][:], ALU.mult, [eb[1]], [SQB[1]])
            msb = rot([0, 1])
            mm(msb, ONES[0][:], SQB[0][:], True, True, [ONES[1], SQB[1]])
            act(ea[0][:], msb[0][:], AF.Sqrt, [msb[1]], [ea[1]], scale=1.0 / 128, bias=EPS)
            recip(ea[0][:], ea[0][:], [ea[1]], [ea[1]])
            stt(YA[0][:, hd, :], eb[0][:], SUBW[0][:, 0:1], ea[0][:], ALU.mult, ALU.mult, [eb[1], SUBW[1], ea[1]], [YA[1]])
        while ssm_i[0] < 66:
            ssm_step()
        for o in range(8):
            act(YSO[0][:, o, :], YSO[0][:, o, :], AF.Gelu, [YSO[1]], [YSO[1]])
            cp("pool", GB[0][:, o, :], YSO[0][:, o, :], [YSO[1]], [GB[1]])
        for fb in range(2):
            wt = load_w("glu", fb)
            for j in range(4):
                fc = fb * 4 + j
                ps = rot([0, 1, 2, 3])
                for dc in range(8):
                    mm(ps, wt[0][:, dc, j * 128:(j + 1) * 128], GB[0][:, dc, :], dc == 0, dc == 7, [wt[1], GB[1]])
                ft = FT[fc % 2]
                act(ft[0][:], ps[0][:], AF.Sigmoid, [ps[1], BGLU[1]], [ft[1]], bias=BGLU[0][:, fc:fc + 1])
                tt("dve", YSB[0][:, fc, :], YSO[0][:, fc, :], ft[0][:], ALU.mult, [YSO[1], ft[1]], [YSB[1]])
        for fb in range(4):
            wt = load_w("in", 8 + fb)
            for j in range(4):
                ps = rot([0, 1, 2, 3])
                for dc in range(DC):
                    mm(ps, wt[0][:, dc, j * 128:(j + 1) * 128], U[0][:, dc, :], dc == 0, dc == DC - 1, [wt[1], U[1]])
                act(SA[0][:, j, :], ps[0][:], AF.Sigmoid, [ps[1]], [SA[1]])
            wt = load_w("a", fb)
            for j in range(4):
                ps = rot([0, 1, 2, 3])
                for dc in range(8):
                    mm(ps, wt[0][:, dc, j * 128:(j + 1) * 128], YA[0][:, dc, :], dc == 0, dc == 7, [wt[1], YA[1]])
                tt("dve", MT[0][:, j, :], ps[0][:], SA[0][:, j, :], ALU.mult, [ps[1], SA[1]], [MT[1]])
            wt = load_w("in", 12 + fb)
            for j in range(4):
                ps = rot([0, 1, 2, 3])
                for dc in range(DC):
                    mm(ps, wt[0][:, dc, j * 128:(j + 1) * 128], U[0][:, dc, :], dc == 0, dc == DC - 1, [wt[1], U[1]])
                act(SA[0][:, j, :], ps[0][:], AF.Sigmoid, [ps[1]], [SA[1]])
            wt = load_w("s", fb)
            for j in range(4):
                ps = rot([0, 1, 2, 3])
                for dc in range(8):
                    mm(ps, wt[0][:, dc, j * 128:(j + 1) * 128], YSB[0][:, dc, :], dc == 0, dc == 7, [wt[1], YSB[1]])
                tt("dve", SA[0][:, j, :], ps[0][:], SA[0][:, j, :], ALU.mult, [ps[1], SA[1]], [SA[1]])
                tt("dve", M[0][:, fb * 4 + j, :], SA[0][:, j, :], MT[0][:, j, :], ALU.add, [SA[1], MT[1]], [M[1]])
        stat = PS[7]
        for fb in range(4):
            wt = load_w("o", fb)
            for j in range(4):
                fc = fb * 4 + j
                ps = rot([0, 1, 2, 3])
                for dc in range(DC):
                    mm(ps, wt[0][:, dc, j * 128:(j + 1) * 128], M[0][:, dc, :], dc == 0, dc == DC - 1, [wt[1], M[1]])
                act(BIG1[0][:, fc, :], ps[0][:], AF.Copy, [ps[1]], [BIG1[1]])
                sq = SQ2[fc % 2]
                act(sq[0][:], ps[0][:], AF.Square, [ps[1]], [sq[1]])
                mm(stat, ONES[0][:], sq[0][:], fc == 0, fc == 15, [ONES[1], sq[1]])
        act(RSTD2[0][:], stat[0][:], AF.Sqrt, [stat[1]], [RSTD2[1]], scale=1.0 / D, bias=EPS)
        recip(RSTD2[0][:], RSTD2[0][:], [RSTD2[1]], [RSTD2[1]])
        S.dma([(H[0][:], xT_own[:, c0:c0 + NT].rearrange("(c p) t -> p c t", p=128))], [], [H[1]], H[1])
        for fc in range(DC):
            stt(BIG1[0][:, fc, :], BIG1[0][:, fc, :], VEC16[0][:, 1, fc:fc + 1], RSTD2[0][:], ALU.mult, ALU.mult,
                [BIG1[1], VEC16[1], RSTD2[1]], [BIG1[1]])
            tt("pool", H[0][:, fc, :], H[0][:, fc, :], BIG1[0][:, fc, :], ALU.add, [H[1], BIG1[1]], [H[1]])
        stat = PS[6]
        for fc in range(DC):
            sq = SQ2[fc % 2]
            act(sq[0][:], H[0][:, fc, :], AF.Square, [H[1]], [sq[1]])
            mm(stat, ONES[0][:], sq[0][:], fc == 0, fc == 15, [ONES[1], sq[1]])
        act(RSTD2[0][:], stat[0][:], AF.Sqrt, [stat[1]], [RSTD2[1]], scale=1.0 / D, bias=EPS)
        recip(RSTD2[0][:], RSTD2[0][:], [RSTD2[1]], [RSTD2[1]])
        for fc in range(DC):
            stt(U[0][:, fc, :], H[0][:, fc, :], VEC16[0][:, 2, fc:fc + 1], RSTD2[0][:], ALU.mult, ALU.mult,
                [H[1], VEC16[1], RSTD2[1]], [U[1]])
        stat = PS[7]
        for half, fbs in enumerate((range(0, 6), range(6, 11))):
            nfc = len(fbs) * 4
            for fb in fbs:
                wg = load_w("g", fb); wu = load_w("u", fb)
                for j in range(4):
                    pg = rot([0, 1]); pu = rot([2, 3])
                    for dc in range(DC):
                        mm(pg, wg[0][:, dc, j * 128:(j + 1) * 128], U[0][:, dc, :], dc == 0, dc == DC - 1, [wg[1], U[1]])
                    for dc in range(DC):
                        mm(pu, wu[0][:, dc, j * 128:(j + 1) * 128], U[0][:, dc, :], dc == 0, dc == DC - 1, [wu[1], U[1]])
                    ft = FT[j % 2]
                    act(ft[0][:], pg[0][:], AF.Silu, [pg[1]], [ft[1]])
                    tt("dve", A[0][:, (fb - fbs[0]) * 4 + j, :], pu[0][:], ft[0][:], ALU.mult, [pu[1], ft[1]], [A[1]])
            dc0 = fbs[0] * 4
            npc = nfc // 2
            for fbo in range(4):
                banks = [PS[2], PS[3], PS[4], PS[5]]
                for pc in range(2):
                    wt = load_w("d", fbo, dc0 + pc * npc, npc)
                    for j in range(4):
                        for dl in range(npc):
                            mm(banks[j], wt[0][:, dl, j * 128:(j + 1) * 128], A[0][:, pc * npc + dl, :],
                               pc == 0 and dl == 0, pc == 1 and dl == npc - 1, [wt[1], A[1]])
                for j in range(4):
                    fc = fbo * 4 + j
                    if half == 0:
                        act(BIG1[0][:, fc, :], banks[j][0][:], AF.Copy, [banks[j][1]], [BIG1[1]])
                    else:
                        tt("dve", BIG1[0][:, fc, :], banks[j][0][:], BIG1[0][:, fc, :], ALU.add, [banks[j][1], BIG1[1]], [BIG1[1]])
                        sq = SQ2[fc % 2]
                        act(sq[0][:], BIG1[0][:, fc, :], AF.Square, [BIG1[1]], [sq[1]])
                        mm(stat, ONES[0][:], sq[0][:], fc == 0, fc == 15, [ONES[1], sq[1]])
        act(RSTD2[0][:], stat[0][:], AF.Sqrt, [stat[1]], [RSTD2[1]], scale=1.0 / D, bias=EPS)
        recip(RSTD2[0][:], RSTD2[0][:], [RSTD2[1]], [RSTD2[1]])
        for fc in range(DC):
            stt(BIG1[0][:, fc, :], BIG1[0][:, fc, :], VEC16[0][:, 3, fc:fc + 1], RSTD2[0][:], ALU.mult, ALU.mult,
                [BIG1[1], VEC16[1], RSTD2[1]], [BIG1[1]])
            tt("pool", BIG1[0][:, fc, :], BIG1[0][:, fc, :], H[0][:, fc, :], ALU.add, [H[1], BIG1[1]], [BIG1[1]])
        S.dma([(outT[:, c0:c0 + NT].rearrange("(c p) t -> p c t", p=128), BIG1[0][:])], [BIG1[1]], [], BIG1[1])
    E = S.E["sp"]
    E.prog.append(([(BIG1[1].dsem, BIG1[1].dcnt)], None, None))

    with nc.allow_non_contiguous_dma(reason="param layouts"), nc.Block() as block:
        @block.tensor
        def _(h): S.replay(h, "pe")

        @block.scalar
        def _(h): S.replay(h, "act")

        @block.vector
        def _(h): S.replay(h, "dve")

        @block.gpsimd
        def _(h): S.replay(h, "pool")

        @block.sync
        def _(h): S.replay(h, "sp")
    stack.close()
    return nc


def host_inputs(NCH, x, w_in, lambda_q1, lambda_k1, lambda_q2, lambda_k2, subln_w, ssm_a_re, ssm_a_im,
                ssm_log_dt, ssm_b_re, ssm_b_im, ssm_c_re, ssm_c_im, ssm_d, w_glu, b_glu, w_attn_branch,
                w_ssm_branch, w_out, norm_mix_pre, norm_mix_post, w_ffn_gate, w_ffn_up, w_ffn_down,
                norm_ffn_pre, norm_ffn_post):
    f32 = np.float32
    bf = ml_dtypes.bfloat16
    c = lambda a: np.ascontiguousarray(np.asarray(a, dtype=f32))
    vec = lambda v: c(np.asarray(v)[0].reshape(-1, 128).T)
    shared = {
        "w_in": c(w_in[0]), "w_glu": c(w_glu[0]), "w_a": c(w_attn_branch[0]), "w_s": c(w_ssm_branch[0]),
        "w_o": c(w_out[0]), "w_g": c(w_ffn_gate[0]), "w_u": c(w_ffn_up[0]), "w_d": c(w_ffn_down[0]),
        "vec16": c(np.stack([vec(norm_mix_pre), vec(norm_mix_post), vec(norm_ffn_pre), vec(norm_ffn_post)], axis=1)),
        "bglu": vec(b_glu), "dvec": c(np.asarray(ssm_d)[0].reshape(8, 128).T),
        "lamv": c(np.broadcast_to(np.stack([np.asarray(v)[0] for v in (lambda_q1, lambda_k1, lambda_q2, lambda_k2)])[None], (128, 4, 64))),
        "subw": c(np.asarray(subln_w)[0].reshape(128, 1)),
        "ident": np.eye(128, dtype=f32).astype(bf),
        "iota": c(np.broadcast_to(np.arange(513, dtype=f32)[None], (128, 513))),
    }
    are, aim, ldt = np.asarray(ssm_a_re)[0], np.asarray(ssm_a_im)[0], np.asarray(ssm_log_dt)[0]
    A_ = lambda m: m.reshape(32, 128).T
    shared["ssmA"] = c(np.stack([A_(are), A_(aim), A_(np.repeat(ldt[:, None], 64, 1))], axis=1))
    bre, bim = np.asarray(ssm_b_re)[0], np.asarray(ssm_b_im)[0]
    cre, cim = np.asarray(ssm_c_re)[0], np.asarray(ssm_c_im)[0]
    P5 = np.zeros((5, 128, 32, 128), f32); C2 = np.zeros((2, 128, 32, 128), f32)
    for q in range(32):
        o = q // 4
        for gl in range(8):
            g = 8 * o + gl
            rows = slice(gl * 16, gl * 16 + 16)
            for two in range(2):
                cols = slice(two * 64, two * 64 + 64)
                P5[0, rows, q, cols] = are[g][None, :]
                P5[1, rows, q, cols] = aim[g][None, :]
                P5[2, rows, q, cols] = ldt[g]
                if gl == 2 * (q % 4) + two:
                    P5[3, rows, q, cols] = bre[g].T
                    P5[4, rows, q, cols] = bim[g].T
                    C2[0, cols, q, rows] = cre[g].T
                    C2[1, cols, q, rows] = cim[g].T
    shared["ssmP"] = P5; shared["ssmC"] = C2
    xs = np.asarray(x, dtype=f32)
    maps = []
    for core in range(8):
        b, h = core // 2, core % 2
        xt = np.ascontiguousarray(xs[b, :NCH * NT].T)
        own = np.concatenate([xt[:, (2 * s + h) * NT:(2 * s + h + 1) * NT] for s in range(NCH // 2)], axis=1)
        mask = np.zeros((128, 2, 4, 512), f32)
        ii = np.arange(128)[:, None]; jj = np.arange(512)[None, :]
        for r in range(2):
            for kt in range(4):
                allowed = ((r - h) * 512 + kt * 128 + ii) <= jj
                mask[:, r, kt, :] = np.where(allowed, 0.0, -30000.0)
        msel = np.zeros((128, 2), f32); msel[:, h] = 1.0
        m = dict(shared)
        m.update({"xT_all": xt, "xT_own": np.ascontiguousarray(own), "mask": mask.astype(bf), "msel": msel})
        maps.append(m)
    return maps


_NC_CACHE = {}


def run(NCH, **inputs):
    maps = host_inputs(NCH, **inputs)
    if NCH not in _NC_CACHE:
        _NC_CACHE[NCH] = build_nc(NCH)
    nc = _NC_CACHE[NCH]
    res = run_bass_kernel_spmd(nc, maps, core_ids=list(range(8)))
    out = np.zeros((4, NCH * NT, D), np.float32)
    for core in range(8):
        b, h = core // 2, core % 2
        oT = res.results[core]["outT"]
        for s in range(NCH // 2):
            out[b, (2 * s + h) * NT:(2 * s + h + 1) * NT, :] = oT[:, s * NT:(s + 1) * NT].T
    return out


def kernel(**inputs):
    return run(16, **inputs)
```

```python
import math, contextlib
import numpy as np
import ml_dtypes
import concourse.bass as bass
import concourse.mybir as mybir
from concourse.bass_utils import run_bass_kernel_spmd

F32, BF16, I32 = mybir.dt.float32, mybir.dt.bfloat16, mybir.dt.int32
AF = mybir.ActivationFunctionType
ALU = mybir.AluOpType
AX = mybir.AxisListType

D = 2048; DC = 16; NT = 512; DFF = 5632; EPS = 1e-6
LAM_INIT = 0.8 - 0.6 * math.exp(0.0)
TWO_PI = 2.0 * math.pi
SIN_SCALE = TWO_PI * (1.0 - 4e-7)


class Res:
    def __init__(s, name, off=None, nbytes=0):
        s.name = name; s.w = None; s.rd = {}; s.al = []; s.dsem = None; s.dcnt = 0
        s.off = off; s.end = None if off is None else off + nbytes


class Eng:
    def __init__(s, name, sem):
        s.name = name; s.sem = sem; s.cnt = 0; s.waited = {}; s.prog = []


class Sched:
    def __init__(s, nc, stack):
        s.nc = nc; s.stack = stack; s.nsem = 0
        s.E = {n: Eng(n, s.newsem("e_" + n)) for n in ("pe", "act", "dve", "pool", "sp")}
        s.sb = []

    def newsem(s, name):
        s.nsem += 1
        return s.stack.enter_context(s.nc.semaphore(name))

    def res(s, name):
        return Res(name)

    def tile(s, name, shape, dt, off):
        esz = 4 if dt in (F32, I32) else 2
        nb = esz * int(np.prod(shape[1:]))
        t = s.nc.alloc_sbuf_tensor_at(name, list(shape), dt, offset=off)
        r = Res(name, off, nb)
        for o in s.sb:
            if o.off < r.end and r.off < o.end:
                o.al.append(r); r.al.append(o)
        s.sb.append(r)
        return t, r

    def _need(s, E, reads, writes):
        need = {}

        def add(sem, val, war):
            if sem is E.sem:
                if E.name == "pe" or war:
                    return
            if E.waited.get(sem, 0) >= val:
                return
            if need.get(sem, 0) < val:
                need[sem] = val

        for r in reads:
            if r.w: add(r.w[0], r.w[1], False)
        for w in writes:
            for x in [w] + w.al:
                if x.w: add(x.w[0], x.w[1], False)
                for sm, v in x.rd.items(): add(sm, v, True)
        for sm, v in need.items():
            E.waited[sm] = v
        return list(need.items())

    def _rec(s, reads, writes, tok):
        for r in reads:
            if r.rd.get(tok[0], 0) < tok[1]: r.rd[tok[0]] = tok[1]
        for w in writes:
            w.w = tok; w.rd = {}

    def op(s, eng, fn, reads=(), writes=()):
        E = s.E[eng]
        waits = s._need(E, reads, writes)
        E.cnt += 1
        s._rec(reads, writes, (E.sem, E.cnt))
        E.prog.append((waits, fn, (E.sem, 1)))

    def dma(s, pairs, reads, writes, semres, q="sp", **kw):
        E = s.E[q]
        waits = s._need(E, reads, writes)
        if semres.dsem is None:
            semres.dsem = s.newsem("d_" + semres.name)
        sem = semres.dsem
        semres.dcnt += 16 * len(pairs)
        s._rec(reads, writes, (sem, semres.dcnt))

        def fn(h, pairs=pairs, sem=sem, kw=kw):
            for o, i in pairs:
                h.dma_start(out=o, in_=i, **kw).then_inc(sem, 16)
            return None
        E.prog.append((waits, fn, None))

    def final_wait(s, q, res_list):
        E = s.E[q]
        waits = s._need(E, res_list, [])
        E.prog.append((waits, None, None))

    def replay(s, h, name):
        for waits, fn, inc in s.E[name].prog:
            for sem, val in waits:
                h.wait_ge(sem, val)
            if fn is None:
                continue
            ins = fn(h)
            if inc is not None:
                ins.then_inc(inc[0], inc[1])


def build_nc(NCH):
    SEQ = NCH * NT; NS = NCH // 2; NOWN = NS * NT
    nc = bass.Bass("TRN2", target_bir_lowering=False)
    stack = contextlib.ExitStack()
    S = Sched(nc, stack)

    def din(name, shape, dt=F32):
        return nc.dram_tensor(name, list(shape), dt, kind="ExternalInput").ap()

    def dscr(name, shape, dt):
        return nc.dram_tensor(name, list(shape), dt, kind="Internal").ap()

    xT_all = din("xT_all", [D, SEQ]); xT_own = din("xT_own", [D, NOWN])
    outT = nc.dram_tensor("outT", [D, NOWN], F32, kind="ExternalOutput").ap()
    w_in = din("w_in", [D, 8192]); w_glu = din("w_glu", [1024, 1024])
    w_a = din("w_a", [1024, D]); w_s = din("w_s", [1024, D]); w_o = din("w_o", [D, D])
    w_g = din("w_g", [D, DFF]); w_u = din("w_u", [D, DFF]); w_d = din("w_d", [DFF, D])
    vec16 = din("vec16", [128, 4, 16])
    bglu_d = din("bglu", [128, 8]); dvec_d = din("dvec", [128, 8])
    lam_d = din("lamv", [128, 4, 64]); subw_d = din("subw", [128, 1]); msel_d = din("msel", [128, 2])
    mask_d = din("mask", [128, 2, 4, 512], BF16)
    ident_d = din("ident", [128, 128], BF16)
    iota_d = din("iota", [128, 513])
    ssmA_d = din("ssmA", [128, 3, 32])
    ssmP_d = din("ssmP", [5, 128, 32, 128])
    ssmC_d = din("ssmC", [2, 128, 32, 128])

    def wscr(name, din_, dout):
        return dscr(name, [dout // 512, 128, din_ // 128, 512], BF16)
    W = {"in": (w_in, D, 8192), "glu": (w_glu, 1024, 1024), "a": (w_a, 1024, D), "s": (w_s, 1024, D),
         "o": (w_o, D, D), "g": (w_g, D, DFF), "u": (w_u, D, DFF), "d": (w_d, DFF, D)}
    WS = {k: wscr("ws_" + k, v[1], v[2]) for k, v in W.items()}
    WR = {k: [[S.res("wr_%s_%d_%d" % (k, fb, pc)) for pc in range((v[1] // 128 + 7) // 8)] for fb in range(v[2] // 512)]
          for k, v in W.items()}
    kscr = dscr("kscr", [8, 128, SEQ], BF16); vscr = dscr("vscr", [8, NCH, 128, 4, 128], BF16)
    kres = [S.res("kscr%d" % i) for i in range(NCH)]; vres = [S.res("vscr%d" % i) for i in range(NCH)]
    tab_s = dscr("tab_s", [32, 128, 2, 513], F32); bt_s = dscr("bt_s", [32, 128, 2, 128], BF16)
    ct_s = dscr("ct_s", [32, 128, 3, 128], BF16)
    tabres = [S.res("tabs%d" % q) for q in range(32)]; btres = [S.res("bts%d" % q) for q in range(32)]; ctres = [S.res("cts%d" % q) for q in range(32)]

    base = 16512
    off = [base]

    def P(name, shape, dt):
        esz = 4 if dt in (F32, I32) else 2
        nb = esz * int(np.prod(shape[1:])); nb = (nb + 31) // 32 * 32
        t = S.tile(name, shape, dt, off[0]); off[0] += nb
        return t
    WSL = [P("wsl%d" % i, [128, 16, 512], BF16) for i in range(3)]
    MASK = P("mask", [128, 2, 4, 512], BF16); IDENT = P("ident", [128, 128], BF16); ONES = P("ones", [128, 128], BF16)
    VEC16 = P("vec16", [128, 4, 16], F32); BGLU = P("bglu", [128, 8], F32); DVEC = P("dvec", [128, 8], F32)
    LAMV = P("lamv", [128, 4, 64], F32); SUBW = P("subw", [128, 1], F32); MSEL = P("msel", [128, 2], F32)
    NEGLAM = P("neglam", [128, 1], F32); LTMP = P("ltmp", [128, 4], F32)
    RA = P("ra", [128, 32], F32); TURNA = P("turna", [128, 32], F32); CS512 = P("cs512", [128, 32, 3], F32)
    CAR = P("car", [128, 32, 2], F32); SSMA = P("ssma", [128, 3, 32], F32); CTMP = P("ctmp", [128, 4], F32)
    YSO = P("yso", [128, 8, 512], F32)
    XB = off[0]
    assert XB % 32 == 0

    def X(name, shape, dt, o):
        return S.tile(name, shape, dt, XB + o)
    XR = [X("xr%d" % i, [128, 512], F32, i * 2048) for i in range(3)]
    SQ = [X("sq%d" % i, [128, 512], BF16, 6144 + i * 1024) for i in range(2)]
    RSTD = X("rstd", [128, 512], F32, 8192)
    UA = X("ua", [128, 16, 512], BF16, 10240)
    KTEV = X("ktev", [128, 8, 512], BF16, 26624); VEV = X("vev", [128, 4, 1024], BF16, 34816)
    SIN = [X("sin%d" % i, [128, 8, 512], BF16, 61440 + i * 8192) for i in range(2)]
    T4 = [X("t%d" % i, [128, 512], F32, 77824 + i * 2048) for i in range(4)]
    WW = [[X("w%d_%d" % (b, j), [128, 513], F32, 86016 + (b * 2 + j) * 2080) for j in range(2)] for b in range(3)]
    XX = [[X("x%d_%d" % (b, j), [128, 512], BF16, 98496 + (b * 4 + j) * 1024) for j in range(4)] for b in range(3)]
    TABS = []
    for i in range(2):
        o = 110784 + i * 5408
        TABS.append((X("tab%d" % i, [128, 2, 513], F32, o), X("bt%d" % i, [128, 2, 128], BF16, o + 4128),
                     X("ct%d" % i, [128, 3, 128], BF16, o + 4640)))
    EPI1 = X("epi1", [128, 512], F32, 121600)
    OE = [X("oe%d" % i, [128, 512], F32, 123648 + i * 2048) for i in range(4)]
    U = X("u", [128, 16, 512], BF16, 0)
    QP = X("qp", [128, 8, 2, 512], BF16, 16384)
    KTW = [X("ktw%d" % i, [128, 512], BF16, 32768 + i * 1024) for i in range(4)]
    VTW = [X("vtw%d" % i, [128, 4, 128], BF16, 36864 + i * 1024) for i in range(4)]
    PT = [X("pt%d" % i, [128, 512], BF16, 40960 + i * 1024) for i in range(4)]
    EP = [X("ep%d" % i, [128, 512], F32, 45056 + i * 2048) for i in range(3)]
    SQB = X("sqb", [128, 512], BF16, 51200)
    YA = X("ya", [128, 8, 512], BF16, 52224); GB = X("gb", [128, 8, 512], BF16, 60416)
    YSB = X("ysb", [128, 8, 512], BF16, 68608)
    SA = X("sa", [128, 4, 512], F32, 76800); MT = X("mt", [128, 4, 512], F32, 84992)
    M = X("m", [128, 16, 512], BF16, 93184)
    BIG1 = X("big1", [128, 16, 512], F32, 16384)
    H = X("h", [128, 16, 512], F32, 49152)
    A = X("a", [128, 24, 512], BF16, 109568)
    RSTD2 = X("rstd2", [128, 512], F32, 84992); SQ2 = [X("sq2_%d" % i, [128, 512], BF16, 87040 + i * 1024) for i in range(2)]
    FT = [X("ft%d" % i, [128, 512], F32, 89088 + i * 2048) for i in range(2)]
    XR2 = [X("xr2_%d" % i, [128, 512], F32, 77824 + i * 2048) for i in range(3)]
    SQ2E = [X("sq2e_%d" % i, [128, 512], BF16, 83968 + i * 1024) for i in range(2)]; RSTD2E = X("rstd2e", [128, 512], F32, 86016)
    assert XB + 109568 + 24576 <= 229344
    ST = [X("st%d" % i, [128, 1024], F32, i * 4096) for i in range(16)]
    STI = X("sti", [128, 1024], I32, 16 * 4096)
    BTST = X("btst", [128, 8, 2, 128], BF16, 17 * 4096)
    CTST = X("ctst", [128, 8, 3, 128], BF16, 18 * 4096)
    TBST = [X("tbst%d" % i, [128, 2, 513], F32, 20 * 4096 + i * 4128) for i in range(2)]
    STG = [X("stg%d" % i, [128, 8, 512], F32, i * 16384) for i in range(4)]
    OBUF = [X("obuf%d" % i, [128, 8, 512], BF16, 94208 + i * 8192) for i in range(3)]
    PS = []
    for i in range(8):
        t = stack.enter_context(nc.psum_tensor("ps%d" % i, [128, 512], F32))
        PS.append((t, S.res("ps%d" % i)))

    def mm(ps, lhsT, rhs, start, stop, reads):
        S.op("pe", lambda h: h.matmul(ps[0][:], lhsT=lhsT, rhs=rhs, start=start, stop=stop),
             reads, [ps[1]])

    def act(out, in_, func, reads, writes, scale=None, bias=None):
        kw = {}
        if scale is not None: kw["scale"] = scale
        if bias is not None: kw["bias"] = bias
        S.op("act", lambda h: h.activation(out=out, in_=in_, func=func, **kw), reads, writes)

    def tt(eng, out, in0, in1, op, reads, writes):
        S.op(eng, lambda h: h.tensor_tensor(out=out, in0=in0, in1=in1, op=op), reads, writes)

    def ts(eng, out, in0, s1, op0, reads, writes, s2=None, op1=None):
        if op1 is None:
            S.op(eng, lambda h: h.tensor_scalar(out=out, in0=in0, scalar1=s1, scalar2=None, op0=op0), reads, writes)
        else:
            S.op(eng, lambda h: h.tensor_scalar(out=out, in0=in0, scalar1=s1, scalar2=s2, op0=op0, op1=op1), reads, writes)

    def stt(out, in0, sc, in1, op0, op1, reads, writes):
        S.op("dve", lambda h: h.scalar_tensor_tensor(out=out, in0=in0, scalar=sc, in1=in1, op0=op0, op1=op1), reads, writes)

    def recip(out, in_, reads, writes):
        S.op("dve", lambda h: h.reciprocal(out=out, in_=in_), reads, writes)

    def cp(eng, out, in_, reads, writes):
        S.op(eng, lambda h: h.tensor_copy(out=out, in_=in_), reads, writes)

    params = S.res("params")

    small = [(VEC16, vec16), (BGLU, bglu_d), (DVEC, dvec_d), (LAMV, lam_d), (SUBW, subw_d), (MSEL, msel_d),
             (MASK, mask_d), (IDENT, ident_d), (SSMA, ssmA_d)]
    S.dma([(t[0][:], d) for t, d in small], [], [t[1] for t, _ in small], params)
    S.op("pool", lambda h: h.memset(ONES[0][:], 1.0), [], [ONES[1]])
    S.op("pool", lambda h: h.memset(CAR[0][:], 0.0), [], [CAR[1]])

    cctr = [0]

    def cast_w(k, fbs):
        src, din_, dout = W[k]
        ndc = din_ // 128
        for fb in fbs:
            sv = src[:, fb * 512:(fb + 1) * 512].rearrange("(dc p) f -> p dc f", p=128)
            for pc in range((ndc + 7) // 8):
                d0 = pc * 8; n = min(8, ndc - d0)
                stg = STG[cctr[0] % 4]
                sl = (WSL + OBUF)[cctr[0] % 6]
                S.dma([(stg[0][:, 0:n, :], sv[:, d0:d0 + n, :])], [], [stg[1]], stg[1])
                e = cctr[0] % 4
                if e == 1:
                    act(sl[0][:, 0:n, :], stg[0][:, 0:n, :], AF.Copy, [stg[1]], [sl[1]])
                elif e == 3:
                    cp("dve", sl[0][:, 0:n, :], stg[0][:, 0:n, :], [stg[1]], [sl[1]])
                else:
                    cp("pool", sl[0][:, 0:n, :], stg[0][:, 0:n, :], [stg[1]], [sl[1]])
                S.dma([(WS[k][fb][:, d0:d0 + n, :], sl[0][:, 0:n, :])], [sl[1]], [WR[k][fb][pc]], sl[1])
                cctr[0] += 1
    cast_w("in", range(2, 8))

    woff = base
    STGL = [S.tile("stgl%d" % i, [128, 8, 512], F32, woff + (1 + i) * 16384) for i in range(2)]
    OUTL = [S.tile("outl%d" % i, [128, 8, 512], BF16, woff + i * 8192) for i in range(2)]
    late = []
    for k, fbs in (("in", [0, 1]), ("glu", range(2)), ("in", range(8, 16)), ("a", range(4)), ("s", range(4)),
                   ("o", range(4)), ("g", range(11)), ("u", range(11)), ("d", range(4))):
        src, din_, dout = W[k]
        ndc = din_ // 128
        for fb in fbs:
            sv = src[:, fb * 512:(fb + 1) * 512].rearrange("(dc p) f -> p dc f", p=128)
            for pc in range((ndc + 7) // 8):
                d0 = pc * 8
                late.append((k, fb, pc, d0, min(8, ndc - d0), sv))
    late_st = [0]; late_fn = [0]

    def late_start():
        i = late_st[0]
        k, fb, pc, d0, n, sv = late[i]
        stg = STGL[i % 2]
        S.dma([(stg[0][:, 0:n, :], sv[:, d0:d0 + n, :])], [], [stg[1]], stg[1])
        late_st[0] += 1

    def late_fin():
        i = late_fn[0]
        k, fb, pc, d0, n, sv = late[i]
        stg = STGL[i % 2]; ot = OUTL[i % 2]
        act(ot[0][:, 0:n, :], stg[0][:, 0:n, :], AF.Copy, [stg[1]], [ot[1]])
        S.dma([(WS[k][fb][:, d0:d0 + n, :], ot[0][:, 0:n, :])], [ot[1]], [WR[k][fb][pc]], ot[1])
        late_fn[0] += 1

    def late_upto(n, drain):
        n = min(n, len(late))
        while late_st[0] < n:
            late_start()
            while late_fn[0] < late_st[0] - 1:
                late_fin()
        if drain:
            while late_fn[0] < late_st[0]:
                late_fin()

    L = LAMV[0]
    tt("dve", L[:, 0, :], L[:, 0, :], L[:, 1, :], ALU.mult, [LAMV[1]], [LAMV[1]])
    tt("dve", L[:, 2, :], L[:, 2, :], L[:, 3, :], ALU.mult, [LAMV[1]], [LAMV[1]])
    S.op("dve", lambda h: h.reduce_sum(out=LTMP[0][:, 0:1], in_=L[:, 0, :], axis=AX.X), [LAMV[1]], [LTMP[1]])
    S.op("dve", lambda h: h.reduce_sum(out=LTMP[0][:, 1:2], in_=L[:, 2, :], axis=AX.X), [LAMV[1]], [LTMP[1]])
    act(LTMP[0][:, 2:4], LTMP[0][:, 0:2], AF.Exp, [LTMP[1]], [LTMP[1]])
    tt("dve", NEGLAM[0][:], LTMP[0][:, 3:4], LTMP[0][:, 2:3], ALU.subtract, [LTMP[1]], [NEGLAM[1]])
    ts("dve", NEGLAM[0][:], NEGLAM[0][:], -LAM_INIT, ALU.add, [NEGLAM[1]], [NEGLAM[1]])
    ts("dve", SUBW[0][:], SUBW[0][:], 1.0 - LAM_INIT, ALU.mult, [SUBW[1]], [SUBW[1]])

    def sincos(t_ap, t_res, n, sin_ap, cos_ap, out_res, tmpf, tmpg, tmpi):
        fa, fr = tmpf[0][:, 0:n], tmpf[1]
        ga, gr = tmpg[0][:, 0:n], tmpg[1]
        ia, ir = tmpi[0][:, 0:n], tmpi[1]
        cp("dve", ia, t_ap, [t_res], [ir])
        cp("dve", ga, ia, [ir], [gr])
        tt("dve", fa, t_ap, ga, ALU.subtract, [t_res, gr], [fr])
        for rep in range(2):
            ts("dve", ga, fa, 0.5, ALU.is_gt, [fr], [gr])
            tt("dve", fa, fa, ga, ALU.subtract, [fr, gr], [fr])
            ts("dve", ga, fa, -0.5, ALU.is_lt, [fr], [gr])
            tt("dve", fa, fa, ga, ALU.add, [fr, gr], [fr])
            if rep == 0:
                act(sin_ap, fa, AF.Sin, [fr], [out_res], scale=SIN_SCALE)
                ts("dve", fa, fa, 0.25, ALU.add, [fr], [fr])
            else:
                act(cos_ap, fa, AF.Sin, [fr], [out_res], scale=SIN_SCALE)

    a = SSMA[0]
    act(a[:, 2, :], a[:, 2, :], AF.Exp, [SSMA[1]], [SSMA[1]])
    tt("dve", a[:, 0, :], a[:, 0, :], a[:, 2, :], ALU.mult, [SSMA[1]], [SSMA[1]])
    act(RA[0][:], a[:, 0, :], AF.Exp, [SSMA[1]], [RA[1]])
    tt("dve", a[:, 1, :], a[:, 1, :], a[:, 2, :], ALU.mult, [SSMA[1]], [SSMA[1]])
    ts("dve", TURNA[0][:], a[:, 1, :], 1.0 / TWO_PI, ALU.mult, [SSMA[1]], [TURNA[1]])

    IOTA = ST[15]
    S.dma([(IOTA[0][:, 0:513], iota_d)], [], [IOTA[1]], IOTA[1], q="act")
    for q in range(32):
        tb = TBST[q % 2]
        ts("dve", ST[12][0][:, 0:513], IOTA[0][:, 0:513], TURNA[0][:, q:q + 1], ALU.mult, [IOTA[1], TURNA[1]], [ST[12][1]])
        sincos(ST[12][0][:, 0:513], ST[12][1], 513, tb[0][:, 1, :], tb[0][:, 0, :], tb[1], ST[13], ST[14], STI)
        cp("dve", CS512[0][:, q, 0:1], tb[0][:, 0, 512:513], [tb[1]], [CS512[1]])
        cp("dve", CS512[0][:, q, 1:2], tb[0][:, 1, 512:513], [tb[1]], [CS512[1]])
        ts("dve", CS512[0][:, q, 2:3], tb[0][:, 1, 512:513], -1.0, ALU.mult, [tb[1]], [CS512[1]])
        S.dma([(tab_s[q], tb[0][:])], [tb[1]], [tabres[q]], tb[1], q="act")

    for qg in range(4):
        qs = slice(qg * 8, qg * 8 + 8)
        lds = []
        for j in range(5):
            lds.append((ST[j][0][:].rearrange("p (q c) -> p q c", q=8), ssmP_d[j, :, qs, :]))
        S.dma(lds, [], [ST[j][1] for j in range(5)], ST[0][1], q="act")
        are, aim, ldt, bre, bim = [ST[j] for j in range(5)]
        f = lambda t: t[0][:]
        act(f(ldt), f(ldt), AF.Exp, [ldt[1]], [ldt[1]])
        tt("dve", f(ST[5]), f(are), f(ldt), ALU.mult, [are[1], ldt[1]], [ST[5][1]])
        act(f(ST[5]), f(ST[5]), AF.Exp, [ST[5][1]], [ST[5][1]])
        tt("dve", f(ST[6]), f(aim), f(ldt), ALU.mult, [aim[1], ldt[1]], [ST[6][1]])
        ts("dve", f(ST[6]), f(ST[6]), 1.0 / TWO_PI, ALU.mult, [ST[6][1]], [ST[6][1]])
        sincos(f(ST[6]), ST[6][1], 1024, f(ST[7]), f(ST[8]), ST[7][1], ST[9], ST[10], STI)
        tt("dve", f(ST[8]), f(ST[8]), f(ST[5]), ALU.mult, [ST[7][1], ST[5][1]], [ST[8][1]])
        ts("dve", f(ST[8]), f(ST[8]), -1.0, ALU.add, [ST[8][1]], [ST[8][1]])
        tt("dve", f(ST[7]), f(ST[7]), f(ST[5]), ALU.mult, [ST[7][1], ST[5][1]], [ST[7][1]])
        tt("dve", f(ST[9]), f(are), f(are), ALU.mult, [are[1]], [ST[9][1]])
        tt("dve", f(ST[10]), f(aim), f(aim), ALU.mult, [aim[1]], [ST[10][1]])
        tt("dve", f(ST[9]), f(ST[9]), f(ST[10]), ALU.add, [ST[9][1], ST[10][1]], [ST[9][1]])
        recip(f(ST[9]), f(ST[9]), [ST[9][1]], [ST[9][1]])
        tt("dve", f(ST[10]), f(ST[8]), f(are), ALU.mult, [ST[8][1], are[1]], [ST[10][1]])
        tt("dve", f(ST[11]), f(ST[7]), f(aim), ALU.mult, [ST[7][1], aim[1]], [ST[11][1]])
        tt("dve", f(ST[10]), f(ST[10]), f(ST[11]), ALU.add, [ST[10][1], ST[11][1]], [ST[10][1]])
        tt("dve", f(ST[10]), f(ST[10]), f(ST[9]), ALU.mult, [ST[10][1], ST[9][1]], [ST[10][1]])
        tt("dve", f(ST[11]), f(ST[7]), f(are), ALU.mult, [ST[7][1], are[1]], [ST[11][1]])
        tt("dve", f(ST[12]), f(ST[8]), f(aim), ALU.mult, [ST[8][1], aim[1]], [ST[12][1]])
        tt("dve", f(ST[11]), f(ST[11]), f(ST[12]), ALU.subtract, [ST[11][1], ST[12][1]], [ST[11][1]])
        tt("dve", f(ST[11]), f(ST[11]), f(ST[9]), ALU.mult, [ST[11][1], ST[9][1]], [ST[11][1]])
        tt("dve", f(ST[12]), f(ST[10]), f(bre), ALU.mult, [ST[10][1], bre[1]], [ST[12][1]])
        tt("dve", f(ST[13]), f(ST[11]), f(bim), ALU.mult, [ST[11][1], bim[1]], [ST[13][1]])
        tt("dve", BTST[0][:, :, 0, :], ST[12][0][:].rearrange("p (q c) -> p q c", q=8), ST[13][0][:].rearrange("p (q c) -> p q c", q=8),
           ALU.subtract, [ST[12][1], ST[13][1]], [BTST[1]])
        tt("dve", f(ST[12]), f(ST[10]), f(bim), ALU.mult, [ST[10][1], bim[1]], [ST[12][1]])
        tt("dve", f(ST[13]), f(ST[11]), f(bre), ALU.mult, [ST[11][1], bre[1]], [ST[13][1]])
        tt("dve", BTST[0][:, :, 1, :], ST[12][0][:].rearrange("p (q c) -> p q c", q=8), ST[13][0][:].rearrange("p (q c) -> p q c", q=8),
           ALU.add, [ST[12][1], ST[13][1]], [BTST[1]])
        S.dma([(bt_s[qs].rearrange("q p r c -> p q r c"), BTST[0][:])], [BTST[1]], [btres[q] for q in range(qg * 8, qg * 8 + 8)], BTST[1], q="act")
        S.dma([(ST[0][0][:].rearrange("p (q c) -> p q c", q=8), ssmC_d[0, :, qs, :]),
               (ST[1][0][:].rearrange("p (q c) -> p q c", q=8), ssmC_d[1, :, qs, :])], [], [ST[0][1], ST[1][1]], ST[0][1], q="act")
        cp("dve", CTST[0][:, :, 0, :], ST[0][0][:].rearrange("p (q c) -> p q c", q=8), [ST[0][1]], [CTST[1]])
        ts("dve", CTST[0][:, :, 1, :], ST[0][0][:].rearrange("p (q c) -> p q c", q=8), -1.0, ALU.mult, [ST[0][1]], [CTST[1]])
        ts("dve", CTST[0][:, :, 2, :], ST[1][0][:].rearrange("p (q c) -> p q c", q=8), -1.0, ALU.mult, [ST[1][1]], [CTST[1]])
        S.dma([(ct_s[qs].rearrange("q p r c -> p q r c"), CTST[0][:])], [CTST[1]], [ctres[q] for q in range(qg * 8, qg * 8 + 8)], CTST[1], q="act")

    wctr = [0]

    def load_w(k, fb, dc0=0, ndc=None):
        src, din_, dout = W[k]
        if ndc is None: ndc = din_ // 128
        sl = WSL[wctr[0] % 3]; wctr[0] += 1
        S.dma([(sl[0][:, 0:ndc, :], WS[k][fb][:, dc0:dc0 + ndc, :])], list(WR[k][fb]), [sl[1]], sl[1])
        return sl

    psctr = [0]

    def rot(banks):
        b = banks[psctr[0] % len(banks)]; psctr[0] += 1
        return PS[b]

    def rms_rstd(pieces, rstd, sqs, bank):
        n = len(pieces)
        for i, (ap, r) in enumerate(pieces):
            sq = sqs[i % 2]
            act(sq[0][:], ap, AF.Square, [r], [sq[1]])
            mm(bank, ONES[0][:], sq[0][:], i == 0, i == n - 1, [ONES[1], sq[1]])
        act(rstd[0][:], bank[0][:], AF.Sqrt, [bank[1]], [rstd[1]], scale=1.0 / D, bias=EPS)
        recip(rstd[0][:], rstd[0][:], [rstd[1]], [rstd[1]])

    def norm_x(xsrc, col0, xr, sqs, rstd, gidx, dst, bank):
        pcs = []
        for dc in range(DC):
            r = xr[dc % 3]
            S.dma([(r[0][:], xsrc[dc * 128:(dc + 1) * 128, col0:col0 + NT])], [], [r[1]], r[1])
            sq = sqs[dc % 2]
            act(sq[0][:], r[0][:], AF.Square, [r[1]], [sq[1]])
            mm(bank, ONES[0][:], sq[0][:], dc == 0, dc == DC - 1, [ONES[1], sq[1]])
        act(rstd[0][:], bank[0][:], AF.Sqrt, [bank[1]], [rstd[1]], scale=1.0 / D, bias=EPS)
        recip(rstd[0][:], rstd[0][:], [rstd[1]], [rstd[1]])
        for dc in range(DC):
            r = xr[dc % 3]
            S.dma([(r[0][:], xsrc[dc * 128:(dc + 1) * 128, col0:col0 + NT])], [], [r[1]], r[1])
            stt(dst[0][:, dc, :], r[0][:], VEC16[0][:, gidx, dc:dc + 1], rstd[0][:], ALU.mult, ALU.mult,
                [r[1], VEC16[1], rstd[1]], [dst[1]])

    def proj_fm(k, fb, act_t, ndc, evac):
        wt = load_w(k, fb)
        for j in range(4):
            ps = rot([0, 1, 2, 3])
            for dc in range(ndc):
                mm(ps, wt[0][:, dc, j * 128:(j + 1) * 128], act_t[0][:, dc, :], dc == 0, dc == ndc - 1, [wt[1], act_t[1]])
            evac(fb * 4 + j, ps)

    for s in range(NS):
        for ci in range(2):
            i = 2 * s + ci
            norm_x(xT_all, i * NT, XR, SQ, RSTD, 0, UA, PS[0])
            for fb in (2, 3):
                proj_fm("in", fb, UA, DC, lambda fc, ps: act(KTEV[0][:, fc - 8, :], ps[0][:], AF.Copy, [ps[1]], [KTEV[1]]))
            S.dma([(kscr[:, :, i * NT:(i + 1) * NT].rearrange("h p t -> p h t"), KTEV[0][:])], [KTEV[1]], [kres[i]], KTEV[1])
            for fbv in (4, 5):
                wt = load_w("in", fbv)
                for t4 in range(4):
                    ps = rot([0, 1, 2, 3])
                    for dc in range(DC):
                        mm(ps, UA[0][:, dc, t4 * 128:(t4 + 1) * 128], wt[0][:, dc, :], dc == 0, dc == DC - 1, [wt[1], UA[1]])
                    cp("dve", VEV[0][:, t4, (fbv - 4) * 512:(fbv - 3) * 512], ps[0][:], [ps[1]], [VEV[1]])
            S.dma([(vscr[:, i].rearrange("h p t e -> p t h e"), VEV[0][:].rearrange("p t (h e) -> p t h e", h=8))],
                  [VEV[1]], [vres[i]], VEV[1])
            for fb in (6, 7):
                proj_fm("in", fb, UA, DC, lambda fc, ps, ci=ci: act(SIN[ci][0][:, fc - 24, :], ps[0][:], AF.Copy, [ps[1]], [SIN[ci][1]]))
        def ssm_ab(q, ci):
            o = q // 4; j = q * 2 + ci
            tb, bt, ct = TABS[q % 2]
            if ci == 0:
                S.dma([(tb[0][:], tab_s[q]), (bt[0][:], bt_s[q]), (ct[0][:], ct_s[q])], [tabres[q], btres[q], ctres[q]],
                      [tb[1], bt[1], ct[1]], tb[1])
            tres = [tb[1], bt[1], ct[1]]
            Zre, Zim = PS[4], PS[5]
            mm(Zre, bt[0][:, 0, :], SIN[ci][0][:, o, :], True, True, tres + [SIN[ci][1]])
            mm(Zim, bt[0][:, 1, :], SIN[ci][0][:, o, :], True, True, tres + [SIN[ci][1]])
            cosv, sinv = tb[0][:, 0, 1:513], tb[0][:, 1, 1:513]
            t1, t2, t3, t4 = T4
            tt("dve", t1[0][:], Zre[0][:], cosv, ALU.mult, [Zre[1]] + tres, [t1[1]])
            tt("dve", t2[0][:], Zim[0][:], sinv, ALU.mult, [Zim[1]] + tres, [t2[1]])
            tt("dve", t3[0][:], Zim[0][:], cosv, ALU.mult, [Zim[1]] + tres, [t3[1]])
            tt("dve", t4[0][:], Zre[0][:], sinv, ALU.mult, [Zre[1]] + tres, [t4[1]])
            tt("dve", t1[0][:], t1[0][:], t2[0][:], ALU.add, [t1[1], t2[1]], [t1[1]])
            tt("dve", t3[0][:], t3[0][:], t4[0][:], ALU.subtract, [t3[1], t4[1]], [t3[1]])
            wre, wim = WW[j % 3]
            rb = RA[0][:, q:q + 1].to_broadcast([128, 512])
            S.op("dve", lambda h: h.tensor_tensor_scan(out=wre[0][:, 1:513], data0=rb, data1=t1[0][:],
                 initial=CAR[0][:, q, 0:1], op0=ALU.mult, op1=ALU.add), [RA[1], t1[1], CAR[1]], [wre[1]])
            S.op("dve", lambda h: h.tensor_tensor_scan(out=wim[0][:, 1:513], data0=rb, data1=t3[0][:],
                 initial=CAR[0][:, q, 1:2], op0=ALU.mult, op1=ALU.add), [RA[1], t3[1], CAR[1]], [wim[1]])
            ts("dve", CTMP[0][:, 0:1], wre[0][:, 512:513], CS512[0][:, q, 0:1], ALU.mult, [wre[1], CS512[1]], [CTMP[1]])
            ts("dve", CTMP[0][:, 1:2], wre[0][:, 512:513], CS512[0][:, q, 1:2], ALU.mult, [wre[1], CS512[1]], [CTMP[1]])
            stt(CAR[0][:, q, 0:1], wim[0][:, 512:513], CS512[0][:, q, 2:3], CTMP[0][:, 0:1], ALU.mult, ALU.add,
                [wim[1], CS512[1], CTMP[1]], [CAR[1]])
            stt(CAR[0][:, q, 1:2], wim[0][:, 512:513], CS512[0][:, q, 0:1], CTMP[0][:, 1:2], ALU.mult, ALU.add,
                [wim[1], CS512[1], CTMP[1]], [CAR[1]])
            x1, x2, x3, x4 = XX[j % 3]
            tt("pool", x1[0][:], wre[0][:, 1:513], cosv, ALU.mult, [wre[1]] + tres, [x1[1]])
            tt("pool", x2[0][:], wim[0][:, 1:513], sinv, ALU.mult, [wim[1]] + tres, [x2[1]])
            tt("pool", x3[0][:], wre[0][:, 1:513], sinv, ALU.mult, [wre[1]] + tres, [x3[1]])
            tt("pool", x4[0][:], wim[0][:, 1:513], cosv, ALU.mult, [wim[1]] + tres, [x4[1]])

        def ssm_c(q, ci):
            o = q // 4; j = q * 2 + ci
            tb, bt, ct = TABS[q % 2]
            tres = [tb[1], bt[1], ct[1]]
            x1, x2, x3, x4 = XX[j % 3]
            Y = PS[6 + ci]
            first = (q % 4 == 0); last = (q % 4 == 3)
            mm(Y, ct[0][:, 0, :], x1[0][:], first, False, tres + [x1[1]])
            mm(Y, ct[0][:, 1, :], x2[0][:], False, False, tres + [x2[1]])
            mm(Y, ct[0][:, 2, :], x3[0][:], False, False, tres + [x3[1]])
            mm(Y, ct[0][:, 2, :], x4[0][:], False, last, tres + [x4[1]])
            if last:
                stt(EPI1[0][:], SIN[ci][0][:, o, :], DVEC[0][:, o:o + 1], Y[0][:], ALU.mult, ALU.add,
                    [SIN[ci][1], DVEC[1], Y[1]], [EPI1[1]])
                if ci == 0:
                    ts("dve", YSO[0][:, o, :], EPI1[0][:], MSEL[0][:, 0:1], ALU.mult, [EPI1[1], MSEL[1]], [YSO[1]])
                else:
                    stt(YSO[0][:, o, :], EPI1[0][:], MSEL[0][:, 1:2], YSO[0][:, o, :], ALU.mult, ALU.add,
                        [EPI1[1], MSEL[1], YSO[1]], [YSO[1]])
        ssm_i = [0]

        def ssm_step():
            j = ssm_i[0]
            if j < 64:
                ssm_ab(j // 2, j % 2)
            if 0 <= j - 2 < 64:
                ssm_c((j - 2) // 2, (j - 2) % 2)
            ssm_i[0] += 1

        c0 = s * NT
        norm_x(xT_own, c0, XR2, SQ2E, RSTD2E, 0, U, PS[0])
        S.op("pool", lambda h: h.memset(QP[0][64:128, :, 0, :], 0.0), [], [QP[1]])
        S.op("pool", lambda h: h.memset(QP[0][0:64, :, 1, :], 0.0), [], [QP[1]])

        def evq(fc, ps):
            act(QP[0][0:64, fc, 0, :], ps[0][0:64, :], AF.Copy, [ps[1]], [QP[1]])
            act(QP[0][64:128, fc, 1, :], ps[0][64:128, :], AF.Copy, [ps[1]], [QP[1]])
        if s == 0:
            late_upto(4, True)
        for fb in (0, 1):
            proj_fm("in", fb, U, DC, evq)
        NK = 2 * s + 2
        kvctr = [0]; ptctr = [0]
        total_tiles = 8 * 2 * NK * 4; tile_ctr = [0]
        Ob, Zb = PS[2], PS[3]
        for hd in range(8):
            tiles = [(c, kc, kt) for c in range(2) for kc in range(NK) for kt in range(4)]
            loaded = {}
            sbank = {}

            def emit_qk(i):
                c, kc, kt = tiles[i]
                if (c, kc) not in loaded:
                    n_ = kvctr[0]; kvctr[0] += 1
                    kw_, vw_ = KTW[n_ % 4], VTW[n_ % 4]
                    S.dma([(kw_[0][:], kscr[hd, :, kc * NT:(kc + 1) * NT])], [kres[kc]], [kw_[1]], kw_[1])
                    S.dma([(vw_[0][:], vscr[hd, kc])], [vres[kc]], [vw_[1]], vw_[1])
                    loaded[(c, kc)] = (kw_, vw_)
                kw_, vw_ = loaded[(c, kc)]
                sp_ = rot([0, 1])
                diag = kc >= NK - 2
                mm(sp_, kw_[0][:, kt * 128:(kt + 1) * 128], QP[0][:, hd, c, :], True, not diag, [kw_[1], QP[1]])
                if diag:
                    mm(sp_, IDENT[0][:], MASK[0][:, kc - (NK - 2), kt, :], False, True, [IDENT[1], MASK[1]])
                sbank[i] = sp_

            emit_qk(0)
            for i in range(len(tiles)):
                if i + 1 < len(tiles):
                    emit_qk(i + 1)
                c, kc, kt = tiles[i]
                kw_, vw_ = loaded[(c, kc)]
                sp_ = sbank.pop(i)
                pt = PT[ptctr[0] % 4]; ptctr[0] += 1
                act(pt[0][:], sp_[0][:], AF.Exp, [sp_[1]], [pt[1]], scale=0.125)
                fst = (kc == 0 and kt == 0); lst = (kc == NK - 1 and kt == 3)
                mm(Ob, vw_[0][:, kt, :], pt[0][:], fst, lst, [vw_[1], pt[1]])
                mm(Zb, ONES[0][:], pt[0][:], fst, lst, [ONES[1], pt[1]])
                if lst:
                    act(OE[2 * c][0][:], Ob[0][:], AF.Copy, [Ob[1]], [OE[2 * c][1]])
                    act(OE[2 * c + 1][0][:], Zb[0][:], AF.Copy, [Zb[1]], [OE[2 * c + 1][1]])
                tile_ctr[0] += 1
                while ssm_i[0] < (tile_ctr[0] * 66) // total_tiles:
                    ssm_step()
                if s == 0:
                    late_upto(4 + (tile_ctr[0] * (len(late) - 3)) // total_tiles, False)
            ea, eb, ec = EP
            recip(OE[1][0][:], OE[1][0][:], [OE[1][1]], [OE[1][1]])
            tt("dve", eb[0][:], OE[0][0][:], OE[1][0][:], ALU.mult, [OE[0][1], OE[1][1]], [eb[1]])
            recip(OE[3][0][:], OE[3][0][:], [OE[3][1]], [OE[3][1]])
            tt("dve", ec[0][:], OE[2][0][:], OE[3][0][:], ALU.mult, [OE[2][1], OE[3][1]], [ec[1]])
            stt(eb[0][:], ec[0][:], NEGLAM[0][:, 0:1], eb[0][:], ALU.mult, ALU.add, [ec[1], NEGLAM[1], eb[1]], [eb[1]])
            tt("dve", SQB[0][:], eb[0][:], eb[0][:], ALU.mult, [eb[1]], [SQB[1]])
            msb = rot([0, 1])
            mm(msb, ONES[0][:], SQB[0][:], True, True, [ONES[1], SQB[1]])
            act(ea[0][:], msb[0][:], AF.Sqrt, [msb[1]], [ea[1]], scale=1.0 / 128, bias=EPS)
            recip(ea[0][:], ea[0][:], [ea[1]], [ea[1]])
            stt(YA[0][:, hd, :], eb[0][:], SUBW[0][:, 0:1], ea[0][:], ALU.mult, ALU.mult, [eb[1], SUBW[1], ea[1]], [YA[1]])
        while ssm_i[0] < 66:
            ssm_step()
        if s == 0:
            late_upto(len(late), True)
        for o in range(8):
            act(YSO[0][:, o, :], YSO[0][:, o, :], AF.Gelu, [YSO[1]], [YSO[1]])
            cp("pool", GB[0][:, o, :], YSO[0][:, o, :], [YSO[1]], [GB[1]])
        for fb in range(2):
            wt = load_w("glu", fb)
            for j in range(4):
                fc = fb * 4 + j
                ps = rot([0, 1, 2, 3])
                for dc in range(8):
                    mm(ps, wt[0][:, dc, j * 128:(j + 1) * 128], GB[0][:, dc, :], dc == 0, dc == 7, [wt[1], GB[1]])
                ft = FT[fc % 2]
                act(ft[0][:], ps[0][:], AF.Sigmoid, [ps[1], BGLU[1]], [ft[1]], bias=BGLU[0][:, fc:fc + 1])
                tt("dve", YSB[0][:, fc, :], YSO[0][:, fc, :], ft[0][:], ALU.mult, [YSO[1], ft[1]], [YSB[1]])
        for fb in range(4):
            wt = load_w("in", 8 + fb)
            for j in range(4):
                ps = rot([0, 1, 2, 3])
                for dc in range(DC):
                    mm(ps, wt[0][:, dc, j * 128:(j + 1) * 128], U[0][:, dc, :], dc == 0, dc == DC - 1, [wt[1], U[1]])
                act(SA[0][:, j, :], ps[0][:], AF.Sigmoid, [ps[1]], [SA[1]])
            wt = load_w("a", fb)
            for j in range(4):
                ps = rot([0, 1, 2, 3])
                for dc in range(8):
                    mm(ps, wt[0][:, dc, j * 128:(j + 1) * 128], YA[0][:, dc, :], dc == 0, dc == 7, [wt[1], YA[1]])
                tt("dve", MT[0][:, j, :], ps[0][:], SA[0][:, j, :], ALU.mult, [ps[1], SA[1]], [MT[1]])
            wt = load_w("in", 12 + fb)
            for j in range(4):
                ps = rot([0, 1, 2, 3])
                for dc in range(DC):
                    mm(ps, wt[0][:, dc, j * 128:(j + 1) * 128], U[0][:, dc, :], dc == 0, dc == DC - 1, [wt[1], U[1]])
                act(SA[0][:, j, :], ps[0][:], AF.Sigmoid, [ps[1]], [SA[1]])
            wt = load_w("s", fb)
            for j in range(4):
                ps = rot([0, 1, 2, 3])
                for dc in range(8):
                    mm(ps, wt[0][:, dc, j * 128:(j + 1) * 128], YSB[0][:, dc, :], dc == 0, dc == 7, [wt[1], YSB[1]])
                tt("dve", SA[0][:, j, :], ps[0][:], SA[0][:, j, :], ALU.mult, [ps[1], SA[1]], [SA[1]])
                tt("dve", M[0][:, fb * 4 + j, :], SA[0][:, j, :], MT[0][:, j, :], ALU.add, [SA[1], MT[1]], [M[1]])
        stat = PS[7]
        for fb in range(4):
            wt = load_w("o", fb)
            for j in range(4):
                fc = fb * 4 + j
                ps = rot([0, 1, 2, 3])
                for dc in range(DC):
                    mm(ps, wt[0][:, dc, j * 128:(j + 1) * 128], M[0][:, dc, :], dc == 0, dc == DC - 1, [wt[1], M[1]])
                act(BIG1[0][:, fc, :], ps[0][:], AF.Copy, [ps[1]], [BIG1[1]])
                sq = SQ2[fc % 2]
                act(sq[0][:], ps[0][:], AF.Square, [ps[1]], [sq[1]])
                mm(stat, ONES[0][:], sq[0][:], fc == 0, fc == 15, [ONES[1], sq[1]])
        act(RSTD2[0][:], stat[0][:], AF.Sqrt, [stat[1]], [RSTD2[1]], scale=1.0 / D, bias=EPS)
        recip(RSTD2[0][:], RSTD2[0][:], [RSTD2[1]], [RSTD2[1]])
        S.dma([(H[0][:], xT_own[:, c0:c0 + NT].rearrange("(c p) t -> p c t", p=128))], [], [H[1]], H[1])
        for fc in range(DC):
            stt(BIG1[0][:, fc, :], BIG1[0][:, fc, :], VEC16[0][:, 1, fc:fc + 1], RSTD2[0][:], ALU.mult, ALU.mult,
                [BIG1[1], VEC16[1], RSTD2[1]], [BIG1[1]])
            tt("pool", H[0][:, fc, :], H[0][:, fc, :], BIG1[0][:, fc, :], ALU.add, [H[1], BIG1[1]], [H[1]])
        stat = PS[6]
        for fc in range(DC):
            sq = SQ2[fc % 2]
            act(sq[0][:], H[0][:, fc, :], AF.Square, [H[1]], [sq[1]])
            mm(stat, ONES[0][:], sq[0][:], fc == 0, fc == 15, [ONES[1], sq[1]])
        act(RSTD2[0][:], stat[0][:], AF.Sqrt, [stat[1]], [RSTD2[1]], scale=1.0 / D, bias=EPS)
        recip(RSTD2[0][:], RSTD2[0][:], [RSTD2[1]], [RSTD2[1]])
        for fc in range(DC):
            stt(U[0][:, fc, :], H[0][:, fc, :], VEC16[0][:, 2, fc:fc + 1], RSTD2[0][:], ALU.mult, ALU.mult,
                [H[1], VEC16[1], RSTD2[1]], [U[1]])
        stat = PS[7]
        for half, fbs in enumerate((range(0, 6), range(6, 11))):
            nfc = len(fbs) * 4
            for fb in fbs:
                wg = load_w("g", fb); wu = load_w("u", fb)
                for j in range(4):
                    pg = rot([0, 1]); pu = rot([2, 3])
                    for dc in range(DC):
                        mm(pg, wg[0][:, dc, j * 128:(j + 1) * 128], U[0][:, dc, :], dc == 0, dc == DC - 1, [wg[1], U[1]])
                    for dc in range(DC):
                        mm(pu, wu[0][:, dc, j * 128:(j + 1) * 128], U[0][:, dc, :], dc == 0, dc == DC - 1, [wu[1], U[1]])
                    ft = FT[j % 2]
                    act(ft[0][:], pg[0][:], AF.Silu, [pg[1]], [ft[1]])
                    tt("dve", A[0][:, (fb - fbs[0]) * 4 + j, :], pu[0][:], ft[0][:], ALU.mult, [pu[1], ft[1]], [A[1]])
            dc0 = fbs[0] * 4
            npc = nfc // 2
            for fbo in range(4):
                banks = [PS[2], PS[3], PS[4], PS[5]]
                for pc in range(2):
                    wt = load_w("d", fbo, dc0 + pc * npc, npc)
                    for j in range(4):
                        for dl in range(npc):
                            mm(banks[j], wt[0][:, dl, j * 128:(j + 1) * 128], A[0][:, pc * npc + dl, :],
                               pc == 0 and dl == 0, pc == 1 and dl == npc - 1, [wt[1], A[1]])
                for j in range(4):
                    fc = fbo * 4 + j
                    if half == 0:
                        act(BIG1[0][:, fc, :], banks[j][0][:], AF.Copy, [banks[j][1]], [BIG1[1]])
                    else:
                        tt("dve", BIG1[0][:, fc, :], banks[j][0][:], BIG1[0][:, fc, :], ALU.add, [banks[j][1], BIG1[1]], [BIG1[1]])
                        sq = SQ2[fc % 2]
                        act(sq[0][:], BIG1[0][:, fc, :], AF.Square, [BIG1[1]], [sq[1]])
                        mm(stat, ONES[0][:], sq[0][:], fc == 0, fc == 15, [ONES[1], sq[1]])
        act(RSTD2[0][:], stat[0][:], AF.Sqrt, [stat[1]], [RSTD2[1]], scale=1.0 / D, bias=EPS)
        recip(RSTD2[0][:], RSTD2[0][:], [RSTD2[1]], [RSTD2[1]])
        for fc in range(DC):
            stt(BIG1[0][:, fc, :], BIG1[0][:, fc, :], VEC16[0][:, 3, fc:fc + 1], RSTD2[0][:], ALU.mult, ALU.mult,
                [BIG1[1], VEC16[1], RSTD2[1]], [BIG1[1]])
            tt("pool", BIG1[0][:, fc, :], BIG1[0][:, fc, :], H[0][:, fc, :], ALU.add, [H[1], BIG1[1]], [BIG1[1]])
        S.dma([(outT[:, c0:c0 + NT].rearrange("(c p) t -> p c t", p=128), BIG1[0][:])], [BIG1[1]], [], BIG1[1])
    E = S.E["sp"]
    E.prog.append(([(BIG1[1].dsem, BIG1[1].dcnt)], None, None))

    with nc.allow_non_contiguous_dma(reason="param layouts"), nc.Block() as block:
        @block.tensor
        def _(h): S.replay(h, "pe")

        @block.scalar
        def _(h): S.replay(h, "act")

        @block.vector
        def _(h): S.replay(h, "dve")

        @block.gpsimd
        def _(h): S.replay(h, "pool")

        @block.sync
        def _(h): S.replay(h, "sp")
    stack.close()
    return nc


def host_inputs(NCH, x, w_in, lambda_q1, lambda_k1, lambda_q2, lambda_k2, subln_w, ssm_a_re, ssm_a_im,
                ssm_log_dt, ssm_b_re, ssm_b_im, ssm_c_re, ssm_c_im, ssm_d, w_glu, b_glu, w_attn_branch,
                w_ssm_branch, w_out, norm_mix_pre, norm_mix_post, w_ffn_gate, w_ffn_up, w_ffn_down,
                norm_ffn_pre, norm_ffn_post):
    f32 = np.float32
    bf = ml_dtypes.bfloat16
    c = lambda a: np.ascontiguousarray(np.asarray(a, dtype=f32))
    vec = lambda v: c(np.asarray(v)[0].reshape(-1, 128).T)
    shared = {
        "w_in": c(w_in[0]), "w_glu": c(w_glu[0]), "w_a": c(w_attn_branch[0]), "w_s": c(w_ssm_branch[0]),
        "w_o": c(w_out[0]), "w_g": c(w_ffn_gate[0]), "w_u": c(w_ffn_up[0]), "w_d": c(w_ffn_down[0]),
        "vec16": c(np.stack([vec(norm_mix_pre), vec(norm_mix_post), vec(norm_ffn_pre), vec(norm_ffn_post)], axis=1)),
        "bglu": vec(b_glu), "dvec": c(np.asarray(ssm_d)[0].reshape(8, 128).T),
        "lamv": c(np.broadcast_to(np.stack([np.asarray(v)[0] for v in (lambda_q1, lambda_k1, lambda_q2, lambda_k2)])[None], (128, 4, 64))),
        "subw": c(np.asarray(subln_w)[0].reshape(128, 1)),
        "ident": np.eye(128, dtype=f32).astype(bf),
        "iota": c(np.broadcast_to(np.arange(513, dtype=f32)[None], (128, 513))),
    }
    are, aim, ldt = np.asarray(ssm_a_re)[0], np.asarray(ssm_a_im)[0], np.asarray(ssm_log_dt)[0]
    A_ = lambda m: m.reshape(32, 128).T
    shared["ssmA"] = c(np.stack([A_(are), A_(aim), A_(np.repeat(ldt[:, None], 64, 1))], axis=1))
    bre, bim = np.asarray(ssm_b_re)[0], np.asarray(ssm_b_im)[0]
    cre, cim = np.asarray(ssm_c_re)[0], np.asarray(ssm_c_im)[0]
    P5 = np.zeros((5, 128, 32, 128), f32); C2 = np.zeros((2, 128, 32, 128), f32)
    for q in range(32):
        o = q // 4
        for gl in range(8):
            g = 8 * o + gl
            rows = slice(gl * 16, gl * 16 + 16)
            for two in range(2):
                cols = slice(two * 64, two * 64 + 64)
                P5[0, rows, q, cols] = are[g][None, :]
                P5[1, rows, q, cols] = aim[g][None, :]
                P5[2, rows, q, cols] = ldt[g]
                if gl == 2 * (q % 4) + two:
                    P5[3, rows, q, cols] = bre[g].T
                    P5[4, rows, q, cols] = bim[g].T
                    C2[0, cols, q, rows] = cre[g].T
                    C2[1, cols, q, rows] = cim[g].T
    shared["ssmP"] = P5; shared["ssmC"] = C2
    xs = np.asarray(x, dtype=f32)
    maps = []
    for core in range(8):
        b, h = core // 2, core % 2
        xt = np.ascontiguousarray(xs[b, :NCH * NT].T)
        own = np.concatenate([xt[:, (2 * s + h) * NT:(2 * s + h + 1) * NT] for s in range(NCH // 2)], axis=1)
        mask = np.zeros((128, 2, 4, 512), f32)
        ii = np.arange(128)[:, None]; jj = np.arange(512)[None, :]
        for r in range(2):
            for kt in range(4):
                allowed = ((r - h) * 512 + kt * 128 + ii) <= jj
                mask[:, r, kt, :] = np.where(allowed, 0.0, -30000.0)
        msel = np.zeros((128, 2), f32); msel[:, h] = 1.0
        m = dict(shared)
        m.update({"xT_all": xt, "xT_own": np.ascontiguousarray(own), "mask": mask.astype(bf), "msel": msel})
        maps.append(m)
    return maps


_NC_CACHE = {}


def run(NCH, **inputs):
    maps = host_inputs(NCH, **inputs)
    if NCH not in _NC_CACHE:
        _NC_CACHE[NCH] = build_nc(NCH)
    nc = _NC_CACHE[NCH]
    res = run_bass_kernel_spmd(nc, maps, core_ids=list(range(8)))
    out = np.zeros((4, NCH * NT, D), np.float32)
    for core in range(8):
        b, h = core // 2, core % 2
        oT = res.results[core]["outT"]
        for s in range(NCH // 2):
            out[b, (2 * s + h) * NT:(2 * s + h + 1) * NT, :] = oT[:, s * NT:(s + 1) * NT].T
    return out


def kernel(**inputs):
    return run(16, **inputs)
```

```python
import math, contextlib
import numpy as np
import ml_dtypes
import concourse.bass as bass
import concourse.mybir as mybir
from concourse.bass_utils import run_bass_kernel_spmd

F32, BF16, I32 = mybir.dt.float32, mybir.dt.bfloat16, mybir.dt.int32
AF = mybir.ActivationFunctionType
ALU = mybir.AluOpType
AX = mybir.AxisListType

D = 2048; DC = 16; NT = 512; DFF = 5632; EPS = 1e-6
LAM_INIT = 0.8 - 0.6 * math.exp(0.0)
TWO_PI = 2.0 * math.pi
SIN_SCALE = TWO_PI * (1.0 - 4e-7)


class Res:
    def __init__(s, name, off=None, nbytes=0):
        s.name = name; s.w = None; s.rd = {}; s.al = []; s.dsem = None; s.dcnt = 0
        s.off = off; s.end = None if off is None else off + nbytes


class Eng:
    def __init__(s, name, sem):
        s.name = name; s.sem = sem; s.cnt = 0; s.waited = {}; s.prog = []


class Sched:
    def __init__(s, nc, stack):
        s.nc = nc; s.stack = stack; s.nsem = 0
        s.E = {n: Eng(n, s.newsem("e_" + n)) for n in ("pe", "act", "dve", "pool", "sp")}
        s.sb = []

    def newsem(s, name):
        s.nsem += 1
        return s.stack.enter_context(s.nc.semaphore(name))

    def res(s, name):
        return Res(name)

    def tile(s, name, shape, dt, off):
        esz = 4 if dt in (F32, I32) else 2
        nb = esz * int(np.prod(shape[1:]))
        t = s.nc.alloc_sbuf_tensor_at(name, list(shape), dt, offset=off)
        r = Res(name, off, nb)
        for o in s.sb:
            if o.off < r.end and r.off < o.end:
                o.al.append(r); r.al.append(o)
        s.sb.append(r)
        return t, r

    def _need(s, E, reads, writes):
        need = {}

        def add(sem, val, war):
            if sem is E.sem:
                if E.name == "pe" or war:
                    return
            if E.waited.get(sem, 0) >= val:
                return
            if need.get(sem, 0) < val:
                need[sem] = val

        for r in reads:
            if r.w: add(r.w[0], r.w[1], False)
        for w in writes:
            for x in [w] + w.al:
                if x.w: add(x.w[0], x.w[1], False)
                for sm, v in x.rd.items(): add(sm, v, True)
        for sm, v in need.items():
            E.waited[sm] = v
        return list(need.items())

    def _rec(s, reads, writes, tok):
        for r in reads:
            if r.rd.get(tok[0], 0) < tok[1]: r.rd[tok[0]] = tok[1]
        for w in writes:
            w.w = tok; w.rd = {}

    def op(s, eng, fn, reads=(), writes=()):
        E = s.E[eng]
        waits = s._need(E, reads, writes)
        E.cnt += 1
        s._rec(reads, writes, (E.sem, E.cnt))
        E.prog.append((waits, fn, (E.sem, 1)))

    def dma(s, pairs, reads, writes, semres, q="sp", **kw):
        E = s.E[q]
        waits = s._need(E, reads, writes)
        if semres.dsem is None:
            semres.dsem = s.newsem("d_" + semres.name)
        sem = semres.dsem
        semres.dcnt += 16 * len(pairs)
        s._rec(reads, writes, (sem, semres.dcnt))

        def fn(h, pairs=pairs, sem=sem, kw=kw):
            for o, i in pairs:
                h.dma_start(out=o, in_=i, **kw).then_inc(sem, 16)
            return None
        E.prog.append((waits, fn, None))

    def final_wait(s, q, res_list):
        E = s.E[q]
        waits = s._need(E, res_list, [])
        E.prog.append((waits, None, None))

    def replay(s, h, name):
        for waits, fn, inc in s.E[name].prog:
            for sem, val in waits:
                h.wait_ge(sem, val)
            if fn is None:
                continue
            ins = fn(h)
            if inc is not None:
                ins.then_inc(inc[0], inc[1])


def build_nc(NCH):
    SEQ = NCH * NT; NS = NCH // 2; NOWN = NS * NT
    nc = bass.Bass("TRN2", target_bir_lowering=False)
    stack = contextlib.ExitStack()
    S = Sched(nc, stack)

    def din(name, shape, dt=F32):
        return nc.dram_tensor(name, list(shape), dt, kind="ExternalInput").ap()

    def dscr(name, shape, dt):
        return nc.dram_tensor(name, list(shape), dt, kind="Internal").ap()

    xT_all = din("xT_all", [D, SEQ]); xT_own = din("xT_own", [D, NOWN])
    outT = nc.dram_tensor("outT", [D, NOWN], F32, kind="ExternalOutput").ap()
    w_in = din("w_in", [D, 8192]); w_glu = din("w_glu", [1024, 1024])
    w_a = din("w_a", [1024, D]); w_s = din("w_s", [1024, D]); w_o = din("w_o", [D, D])
    w_g = din("w_g", [D, DFF]); w_u = din("w_u", [D, DFF]); w_d = din("w_d", [DFF, D])
    vec16 = din("vec16", [128, 4, 16])
    bglu_d = din("bglu", [128, 8]); dvec_d = din("dvec", [128, 8])
    lam_d = din("lamv", [128, 4, 64]); subw_d = din("subw", [128, 1]); msel_d = din("msel", [128, 2])
    mask_d = din("mask", [128, 2, 4, 512], BF16)
    ident_d = din("ident", [128, 128], BF16)
    iota_d = din("iota", [128, 513])
    ssmA_d = din("ssmA", [128, 3, 32])
    ssmP_d = din("ssmP", [5, 128, 32, 128])
    ssmC_d = din("ssmC", [2, 128, 32, 128])

    def wscr(name, din_, dout):
        return dscr(name, [dout // 512, 128, din_ // 128, 512], BF16)
    W = {"in": (w_in, D, 8192), "glu": (w_glu, 1024, 1024), "a": (w_a, 1024, D), "s": (w_s, 1024, D),
         "o": (w_o, D, D), "g": (w_g, D, DFF), "u": (w_u, D, DFF), "d": (w_d, DFF, D)}
    WS = {k: wscr("ws_" + k, v[1], v[2]) for k, v in W.items()}
    WR = {k: [[S.res("wr_%s_%d_%d" % (k, fb, pc)) for pc in range((v[1] // 128 + 7) // 8)] for fb in range(v[2] // 512)]
          for k, v in W.items()}
    kscr = dscr("kscr", [8, 128, SEQ], BF16); vscr = dscr("vscr", [8, NCH, 128, 4, 128], BF16)
    kres = [S.res("kscr%d" % i) for i in range(NCH)]; vres = [S.res("vscr%d" % i) for i in range(NCH)]
    tab_s = dscr("tab_s", [32, 128, 2, 513], F32); bt_s = dscr("bt_s", [32, 128, 2, 128], BF16)
    ct_s = dscr("ct_s", [32, 128, 3, 128], BF16)
    tabres = [S.res("tabs%d" % q) for q in range(32)]; btres = [S.res("bts%d" % q) for q in range(32)]; ctres = [S.res("cts%d" % q) for q in range(32)]

    base = 16512
    off = [base]

    def P(name, shape, dt):
        esz = 4 if dt in (F32, I32) else 2
        nb = esz * int(np.prod(shape[1:])); nb = (nb + 31) // 32 * 32
        t = S.tile(name, shape, dt, off[0]); off[0] += nb
        return t
    WSL = [P("wsl%d" % i, [128, 16, 512], BF16) for i in range(3)]
    MASK = P("mask", [128, 2, 4, 512], BF16); IDENT = P("ident", [128, 128], BF16); ONES = P("ones", [128, 128], BF16)
    VEC16 = P("vec16", [128, 4, 16], F32); BGLU = P("bglu", [128, 8], F32); DVEC = P("dvec", [128, 8], F32)
    LAMV = P("lamv", [128, 4, 64], F32); SUBW = P("subw", [128, 1], F32); MSEL = P("msel", [128, 2], F32)
    NEGLAM = P("neglam", [128, 1], F32); LTMP = P("ltmp", [128, 4], F32)
    RA = P("ra", [128, 32], F32); TURNA = P("turna", [128, 32], F32); CS512 = P("cs512", [128, 32, 3], F32)
    CAR = P("car", [128, 32, 2], F32); SSMA = P("ssma", [128, 3, 32], F32); CTMP = P("ctmp", [128, 4], F32)
    YSO = P("yso", [128, 8, 512], F32)
    XB = off[0]
    assert XB % 32 == 0

    def X(name, shape, dt, o):
        return S.tile(name, shape, dt, XB + o)
    XR = [X("xr%d" % i, [128, 512], F32, i * 2048) for i in range(3)]
    SQ = [X("sq%d" % i, [128, 512], BF16, 6144 + i * 1024) for i in range(2)]
    RSTD = X("rstd", [128, 512], F32, 8192)
    UA = X("ua", [128, 16, 512], BF16, 10240)
    KTEV = X("ktev", [128, 8, 512], BF16, 26624); VEV = X("vev", [128, 4, 1024], BF16, 34816)
    SIN = [X("sin%d" % i, [128, 8, 512], BF16, 61440 + i * 8192) for i in range(2)]
    T4 = [X("t%d" % i, [128, 512], F32, 77824 + i * 2048) for i in range(4)]
    WW = [[X("w%d_%d" % (b, j), [128, 513], F32, 86016 + (b * 2 + j) * 2080) for j in range(2)] for b in range(3)]
    XX = [[X("x%d_%d" % (b, j), [128, 512], BF16, 98496 + (b * 4 + j) * 1024) for j in range(4)] for b in range(3)]
    TABS = []
    for i in range(2):
        o = 110784 + i * 5408
        TABS.append((X("tab%d" % i, [128, 2, 513], F32, o), X("bt%d" % i, [128, 2, 128], BF16, o + 4128),
                     X("ct%d" % i, [128, 3, 128], BF16, o + 4640)))
    EPI1 = X("epi1", [128, 512], F32, 121600)
    OE = [X("oe%d" % i, [128, 512], F32, 123648 + i * 2048) for i in range(4)]
    U = X("u", [128, 16, 512], BF16, 0)
    QP = X("qp", [128, 8, 2, 512], BF16, 16384)
    KTW = [X("ktw%d" % i, [128, 512], BF16, 32768 + i * 1024) for i in range(4)]
    VTW = [X("vtw%d" % i, [128, 4, 128], BF16, 36864 + i * 1024) for i in range(4)]
    PT = [X("pt%d" % i, [128, 512], BF16, 40960 + i * 1024) for i in range(4)]
    EP = [X("ep%d" % i, [128, 512], F32, 45056 + i * 2048) for i in range(3)]
    SQB = X("sqb", [128, 512], BF16, 51200)
    YA = X("ya", [128, 8, 512], BF16, 52224); GB = X("gb", [128, 8, 512], BF16, 60416)
    YSB = X("ysb", [128, 8, 512], BF16, 68608)
    SA = X("sa", [128, 4, 512], F32, 76800); MT = X("mt", [128, 4, 512], F32, 84992)
    M = X("m", [128, 16, 512], BF16, 93184)
    BIG1 = X("big1", [128, 16, 512], F32, 16384)
    H = X("h", [128, 16, 512], F32, 49152)
    A = X("a", [128, 24, 512], BF16, 109568)
    RSTD2 = X("rstd2", [128, 512], F32, 84992); SQ2 = [X("sq2_%d" % i, [128, 512], BF16, 87040 + i * 1024) for i in range(2)]
    FT = [X("ft%d" % i, [128, 512], F32, 89088 + i * 2048) for i in range(2)]
    XR2 = [X("xr2_%d" % i, [128, 512], F32, 77824 + i * 2048) for i in range(3)]
    SQ2E = [X("sq2e_%d" % i, [128, 512], BF16, 83968 + i * 1024) for i in range(2)]; RSTD2E = X("rstd2e", [128, 512], F32, 86016)
    assert XB + 109568 + 24576 <= 229344
    ST = [X("st%d" % i, [128, 1024], F32, i * 4096) for i in range(16)]
    STI = X("sti", [128, 1024], I32, 16 * 4096)
    BTST = X("btst", [128, 8, 2, 128], BF16, 17 * 4096)
    CTST = X("ctst", [128, 8, 3, 128], BF16, 18 * 4096)
    TBST = [X("tbst%d" % i, [128, 2, 513], F32, 20 * 4096 + i * 4128) for i in range(2)]
    STG = [X("stg%d" % i, [128, 8, 512], F32, i * 16384) for i in range(4)]
    OBUF = [X("obuf%d" % i, [128, 8, 512], BF16, 94208 + i * 8192) for i in range(3)]
    PS = []
    for i in range(8):
        t = stack.enter_context(nc.psum_tensor("ps%d" % i, [128, 512], F32))
        PS.append((t, S.res("ps%d" % i)))

    def mm(ps, lhsT, rhs, start, stop, reads):
        S.op("pe", lambda h: h.matmul(ps[0][:], lhsT=lhsT, rhs=rhs, start=start, stop=stop),
             reads, [ps[1]])

    def act(out, in_, func, reads, writes, scale=None, bias=None):
        kw = {}
        if scale is not None: kw["scale"] = scale
        if bias is not None: kw["bias"] = bias
        S.op("act", lambda h: h.activation(out=out, in_=in_, func=func, **kw), reads, writes)

    def tt(eng, out, in0, in1, op, reads, writes):
        S.op(eng, lambda h: h.tensor_tensor(out=out, in0=in0, in1=in1, op=op), reads, writes)

    def ts(eng, out, in0, s1, op0, reads, writes, s2=None, op1=None):
        if op1 is None:
            S.op(eng, lambda h: h.tensor_scalar(out=out, in0=in0, scalar1=s1, scalar2=None, op0=op0), reads, writes)
        else:
            S.op(eng, lambda h: h.tensor_scalar(out=out, in0=in0, scalar1=s1, scalar2=s2, op0=op0, op1=op1), reads, writes)

    def stt(out, in0, sc, in1, op0, op1, reads, writes):
        S.op("dve", lambda h: h.scalar_tensor_tensor(out=out, in0=in0, scalar=sc, in1=in1, op0=op0, op1=op1), reads, writes)

    def recip(out, in_, reads, writes):
        S.op("dve", lambda h: h.reciprocal(out=out, in_=in_), reads, writes)

    def recipf(out, in_, reads, writes):
        S.op("dve", lambda h: h.reciprocal_approx_fast(out=out, in_=in_), reads, writes)

    def cp(eng, out, in_, reads, writes):
        S.op(eng, lambda h: h.tensor_copy(out=out, in_=in_), reads, writes)

    params = S.res("params")

    small = [(VEC16, vec16), (BGLU, bglu_d), (DVEC, dvec_d), (LAMV, lam_d), (SUBW, subw_d), (MSEL, msel_d),
             (MASK, mask_d), (IDENT, ident_d), (SSMA, ssmA_d)]
    S.dma([(t[0][:], d) for t, d in small], [], [t[1] for t, _ in small], params)
    S.op("pool", lambda h: h.memset(ONES[0][:], 1.0), [], [ONES[1]])
    S.op("pool", lambda h: h.memset(CAR[0][:], 0.0), [], [CAR[1]])

    cctr = [0]

    def cast_w(k, fbs):
        src, din_, dout = W[k]
        ndc = din_ // 128
        for fb in fbs:
            sv = src[:, fb * 512:(fb + 1) * 512].rearrange("(dc p) f -> p dc f", p=128)
            for pc in range((ndc + 7) // 8):
                d0 = pc * 8; n = min(8, ndc - d0)
                stg = STG[cctr[0] % 4]
                sl = (WSL + OBUF)[cctr[0] % 6]
                S.dma([(stg[0][:, 0:n, :], sv[:, d0:d0 + n, :])], [], [stg[1]], stg[1])
                e = cctr[0] % 4
                if e == 1:
                    act(sl[0][:, 0:n, :], stg[0][:, 0:n, :], AF.Copy, [stg[1]], [sl[1]])
                elif e == 3:
                    cp("dve", sl[0][:, 0:n, :], stg[0][:, 0:n, :], [stg[1]], [sl[1]])
                else:
                    cp("pool", sl[0][:, 0:n, :], stg[0][:, 0:n, :], [stg[1]], [sl[1]])
                S.dma([(WS[k][fb][:, d0:d0 + n, :], sl[0][:, 0:n, :])], [sl[1]], [WR[k][fb][pc]], sl[1])
                cctr[0] += 1
    cast_w("in", range(2, 8))

    woff = base
    STGL = [S.tile("stgl%d" % i, [128, 8, 512], F32, woff + (1 + i) * 16384) for i in range(2)]
    OUTL = [S.tile("outl%d" % i, [128, 8, 512], BF16, woff + i * 8192) for i in range(2)]
    late = []
    for k, fbs in (("in", [0, 1]), ("glu", range(2)), ("in", range(8, 16)), ("a", range(4)), ("s", range(4)),
                   ("o", range(4)), ("g", range(11)), ("u", range(11)), ("d", range(4))):
        src, din_, dout = W[k]
        ndc = din_ // 128
        for fb in fbs:
            sv = src[:, fb * 512:(fb + 1) * 512].rearrange("(dc p) f -> p dc f", p=128)
            for pc in range((ndc + 7) // 8):
                d0 = pc * 8
                late.append((k, fb, pc, d0, min(8, ndc - d0), sv))
    late_st = [0]; late_fn = [0]

    def late_start():
        i = late_st[0]
        k, fb, pc, d0, n, sv = late[i]
        stg = STGL[i % 2]
        S.dma([(stg[0][:, 0:n, :], sv[:, d0:d0 + n, :])], [], [stg[1]], stg[1])
        late_st[0] += 1

    def late_fin():
        i = late_fn[0]
        k, fb, pc, d0, n, sv = late[i]
        stg = STGL[i % 2]; ot = OUTL[i % 2]
        act(ot[0][:, 0:n, :], stg[0][:, 0:n, :], AF.Copy, [stg[1]], [ot[1]])
        S.dma([(WS[k][fb][:, d0:d0 + n, :], ot[0][:, 0:n, :])], [ot[1]], [WR[k][fb][pc]], ot[1])
        late_fn[0] += 1

    def late_upto(n, drain):
        n = min(n, len(late))
        while late_st[0] < n:
            late_start()
            while late_fn[0] < late_st[0] - 1:
                late_fin()
        if drain:
            while late_fn[0] < late_st[0]:
                late_fin()

    L = LAMV[0]
    tt("dve", L[:, 0, :], L[:, 0, :], L[:, 1, :], ALU.mult, [LAMV[1]], [LAMV[1]])
    tt("dve", L[:, 2, :], L[:, 2, :], L[:, 3, :], ALU.mult, [LAMV[1]], [LAMV[1]])
    S.op("dve", lambda h: h.reduce_sum(out=LTMP[0][:, 0:1], in_=L[:, 0, :], axis=AX.X), [LAMV[1]], [LTMP[1]])
    S.op("dve", lambda h: h.reduce_sum(out=LTMP[0][:, 1:2], in_=L[:, 2, :], axis=AX.X), [LAMV[1]], [LTMP[1]])
    act(LTMP[0][:, 2:4], LTMP[0][:, 0:2], AF.Exp, [LTMP[1]], [LTMP[1]])
    tt("dve", NEGLAM[0][:], LTMP[0][:, 3:4], LTMP[0][:, 2:3], ALU.subtract, [LTMP[1]], [NEGLAM[1]])
    ts("dve", NEGLAM[0][:], NEGLAM[0][:], -LAM_INIT, ALU.add, [NEGLAM[1]], [NEGLAM[1]])
    ts("dve", SUBW[0][:], SUBW[0][:], 1.0 - LAM_INIT, ALU.mult, [SUBW[1]], [SUBW[1]])

    def sincos(t_ap, t_res, n, sin_ap, cos_ap, out_res, tmpf, tmpg, tmpi):
        fa, fr = tmpf[0][:, 0:n], tmpf[1]
        ga, gr = tmpg[0][:, 0:n], tmpg[1]
        ia, ir = tmpi[0][:, 0:n], tmpi[1]
        cp("dve", ia, t_ap, [t_res], [ir])
        cp("dve", ga, ia, [ir], [gr])
        tt("dve", fa, t_ap, ga, ALU.subtract, [t_res, gr], [fr])
        for rep in range(2):
            ts("dve", ga, fa, 0.5, ALU.is_gt, [fr], [gr])
            tt("dve", fa, fa, ga, ALU.subtract, [fr, gr], [fr])
            ts("dve", ga, fa, -0.5, ALU.is_lt, [fr], [gr])
            tt("dve", fa, fa, ga, ALU.add, [fr, gr], [fr])
            if rep == 0:
                act(sin_ap, fa, AF.Sin, [fr], [out_res], scale=SIN_SCALE)
                ts("dve", fa, fa, 0.25, ALU.add, [fr], [fr])
            else:
                act(cos_ap, fa, AF.Sin, [fr], [out_res], scale=SIN_SCALE)

    a = SSMA[0]
    act(a[:, 2, :], a[:, 2, :], AF.Exp, [SSMA[1]], [SSMA[1]])
    tt("dve", a[:, 0, :], a[:, 0, :], a[:, 2, :], ALU.mult, [SSMA[1]], [SSMA[1]])
    act(RA[0][:], a[:, 0, :], AF.Exp, [SSMA[1]], [RA[1]])
    tt("dve", a[:, 1, :], a[:, 1, :], a[:, 2, :], ALU.mult, [SSMA[1]], [SSMA[1]])
    ts("dve", TURNA[0][:], a[:, 1, :], 1.0 / TWO_PI, ALU.mult, [SSMA[1]], [TURNA[1]])

    IOTA = ST[15]
    S.dma([(IOTA[0][:, 0:513], iota_d)], [], [IOTA[1]], IOTA[1], q="act")
    for q in range(32):
        tb = TBST[q % 2]
        ts("dve", ST[12][0][:, 0:513], IOTA[0][:, 0:513], TURNA[0][:, q:q + 1], ALU.mult, [IOTA[1], TURNA[1]], [ST[12][1]])
        sincos(ST[12][0][:, 0:513], ST[12][1], 513, tb[0][:, 1, :], tb[0][:, 0, :], tb[1], ST[13], ST[14], STI)
        cp("dve", CS512[0][:, q, 0:1], tb[0][:, 0, 512:513], [tb[1]], [CS512[1]])
        cp("dve", CS512[0][:, q, 1:2], tb[0][:, 1, 512:513], [tb[1]], [CS512[1]])
        ts("dve", CS512[0][:, q, 2:3], tb[0][:, 1, 512:513], -1.0, ALU.mult, [tb[1]], [CS512[1]])
        S.dma([(tab_s[q], tb[0][:])], [tb[1]], [tabres[q]], tb[1], q="act")

    for qg in range(4):
        qs = slice(qg * 8, qg * 8 + 8)
        lds = []
        for j in range(5):
            lds.append((ST[j][0][:].rearrange("p (q c) -> p q c", q=8), ssmP_d[j, :, qs, :]))
        S.dma(lds, [], [ST[j][1] for j in range(5)], ST[0][1], q="act")
        are, aim, ldt, bre, bim = [ST[j] for j in range(5)]
        f = lambda t: t[0][:]
        act(f(ldt), f(ldt), AF.Exp, [ldt[1]], [ldt[1]])
        tt("dve", f(ST[5]), f(are), f(ldt), ALU.mult, [are[1], ldt[1]], [ST[5][1]])
        act(f(ST[5]), f(ST[5]), AF.Exp, [ST[5][1]], [ST[5][1]])
        tt("dve", f(ST[6]), f(aim), f(ldt), ALU.mult, [aim[1], ldt[1]], [ST[6][1]])
        ts("dve", f(ST[6]), f(ST[6]), 1.0 / TWO_PI, ALU.mult, [ST[6][1]], [ST[6][1]])
        sincos(f(ST[6]), ST[6][1], 1024, f(ST[7]), f(ST[8]), ST[7][1], ST[9], ST[10], STI)
        tt("dve", f(ST[8]), f(ST[8]), f(ST[5]), ALU.mult, [ST[7][1], ST[5][1]], [ST[8][1]])
        ts("dve", f(ST[8]), f(ST[8]), -1.0, ALU.add, [ST[8][1]], [ST[8][1]])
        tt("dve", f(ST[7]), f(ST[7]), f(ST[5]), ALU.mult, [ST[7][1], ST[5][1]], [ST[7][1]])
        tt("dve", f(ST[9]), f(are), f(are), ALU.mult, [are[1]], [ST[9][1]])
        tt("dve", f(ST[10]), f(aim), f(aim), ALU.mult, [aim[1]], [ST[10][1]])
        tt("dve", f(ST[9]), f(ST[9]), f(ST[10]), ALU.add, [ST[9][1], ST[10][1]], [ST[9][1]])
        recip(f(ST[9]), f(ST[9]), [ST[9][1]], [ST[9][1]])
        tt("dve", f(ST[10]), f(ST[8]), f(are), ALU.mult, [ST[8][1], are[1]], [ST[10][1]])
        tt("dve", f(ST[11]), f(ST[7]), f(aim), ALU.mult, [ST[7][1], aim[1]], [ST[11][1]])
        tt("dve", f(ST[10]), f(ST[10]), f(ST[11]), ALU.add, [ST[10][1], ST[11][1]], [ST[10][1]])
        tt("dve", f(ST[10]), f(ST[10]), f(ST[9]), ALU.mult, [ST[10][1], ST[9][1]], [ST[10][1]])
        tt("dve", f(ST[11]), f(ST[7]), f(are), ALU.mult, [ST[7][1], are[1]], [ST[11][1]])
        tt("dve", f(ST[12]), f(ST[8]), f(aim), ALU.mult, [ST[8][1], aim[1]], [ST[12][1]])
        tt("dve", f(ST[11]), f(ST[11]), f(ST[12]), ALU.subtract, [ST[11][1], ST[12][1]], [ST[11][1]])
        tt("dve", f(ST[11]), f(ST[11]), f(ST[9]), ALU.mult, [ST[11][1], ST[9][1]], [ST[11][1]])
        tt("dve", f(ST[12]), f(ST[10]), f(bre), ALU.mult, [ST[10][1], bre[1]], [ST[12][1]])
        tt("dve", f(ST[13]), f(ST[11]), f(bim), ALU.mult, [ST[11][1], bim[1]], [ST[13][1]])
        tt("dve", BTST[0][:, :, 0, :], ST[12][0][:].rearrange("p (q c) -> p q c", q=8), ST[13][0][:].rearrange("p (q c) -> p q c", q=8),
           ALU.subtract, [ST[12][1], ST[13][1]], [BTST[1]])
        tt("dve", f(ST[12]), f(ST[10]), f(bim), ALU.mult, [ST[10][1], bim[1]], [ST[12][1]])
        tt("dve", f(ST[13]), f(ST[11]), f(bre), ALU.mult, [ST[11][1], bre[1]], [ST[13][1]])
        tt("dve", BTST[0][:, :, 1, :], ST[12][0][:].rearrange("p (q c) -> p q c", q=8), ST[13][0][:].rearrange("p (q c) -> p q c", q=8),
           ALU.add, [ST[12][1], ST[13][1]], [BTST[1]])
        S.dma([(bt_s[qs].rearrange("q p r c -> p q r c"), BTST[0][:])], [BTST[1]], [btres[q] for q in range(qg * 8, qg * 8 + 8)], BTST[1], q="act")
        S.dma([(ST[0][0][:].rearrange("p (q c) -> p q c", q=8), ssmC_d[0, :, qs, :]),
               (ST[1][0][:].rearrange("p (q c) -> p q c", q=8), ssmC_d[1, :, qs, :])], [], [ST[0][1], ST[1][1]], ST[0][1], q="act")
        cp("dve", CTST[0][:, :, 0, :], ST[0][0][:].rearrange("p (q c) -> p q c", q=8), [ST[0][1]], [CTST[1]])
        ts("dve", CTST[0][:, :, 1, :], ST[0][0][:].rearrange("p (q c) -> p q c", q=8), -1.0, ALU.mult, [ST[0][1]], [CTST[1]])
        ts("dve", CTST[0][:, :, 2, :], ST[1][0][:].rearrange("p (q c) -> p q c", q=8), -1.0, ALU.mult, [ST[1][1]], [CTST[1]])
        S.dma([(ct_s[qs].rearrange("q p r c -> p q r c"), CTST[0][:])], [CTST[1]], [ctres[q] for q in range(qg * 8, qg * 8 + 8)], CTST[1], q="act")

    wctr = [0]

    def load_w(k, fb, dc0=0, ndc=None):
        src, din_, dout = W[k]
        if ndc is None: ndc = din_ // 128
        sl = WSL[wctr[0] % 3]; wctr[0] += 1
        S.dma([(sl[0][:, 0:ndc, :], WS[k][fb][:, dc0:dc0 + ndc, :])], list(WR[k][fb]), [sl[1]], sl[1])
        return sl

    psctr = [0]

    def rot(banks):
        b = banks[psctr[0] % len(banks)]; psctr[0] += 1
        return PS[b]

    def rms_rstd(pieces, rstd, sqs, bank):
        n = len(pieces)
        for i, (ap, r) in enumerate(pieces):
            sq = sqs[i % 2]
            act(sq[0][:], ap, AF.Square, [r], [sq[1]])
            mm(bank, ONES[0][:], sq[0][:], i == 0, i == n - 1, [ONES[1], sq[1]])
        act(rstd[0][:], bank[0][:], AF.Sqrt, [bank[1]], [rstd[1]], scale=1.0 / D, bias=EPS)
        recip(rstd[0][:], rstd[0][:], [rstd[1]], [rstd[1]])

    def norm_x(xsrc, col0, xr, sqs, rstd, gidx, dst, bank):
        pcs = []
        for dc in range(DC):
            r = xr[dc % 3]
            S.dma([(r[0][:], xsrc[dc * 128:(dc + 1) * 128, col0:col0 + NT])], [], [r[1]], r[1])
            sq = sqs[dc % 2]
            act(sq[0][:], r[0][:], AF.Square, [r[1]], [sq[1]])
            mm(bank, ONES[0][:], sq[0][:], dc == 0, dc == DC - 1, [ONES[1], sq[1]])
        act(rstd[0][:], bank[0][:], AF.Sqrt, [bank[1]], [rstd[1]], scale=1.0 / D, bias=EPS)
        recip(rstd[0][:], rstd[0][:], [rstd[1]], [rstd[1]])
        for dc in range(DC):
            r = xr[dc % 3]
            S.dma([(r[0][:], xsrc[dc * 128:(dc + 1) * 128, col0:col0 + NT])], [], [r[1]], r[1])
            stt(dst[0][:, dc, :], r[0][:], VEC16[0][:, gidx, dc:dc + 1], rstd[0][:], ALU.mult, ALU.mult,
                [r[1], VEC16[1], rstd[1]], [dst[1]])

    def proj_fm(k, fb, act_t, ndc, evac):
        wt = load_w(k, fb)
        for j in range(4):
            ps = rot([0, 1, 2, 3])
            for dc in range(ndc):
                mm(ps, wt[0][:, dc, j * 128:(j + 1) * 128], act_t[0][:, dc, :], dc == 0, dc == ndc - 1, [wt[1], act_t[1]])
            evac(fb * 4 + j, ps)

    for s in range(NS):
        for ci in range(2):
            i = 2 * s + ci
            norm_x(xT_all, i * NT, XR, SQ, RSTD, 0, UA, PS[0])
            for fb in (2, 3):
                proj_fm("in", fb, UA, DC, lambda fc, ps: act(KTEV[0][:, fc - 8, :], ps[0][:], AF.Copy, [ps[1]], [KTEV[1]]))
            S.dma([(kscr[:, :, i * NT:(i + 1) * NT].rearrange("h p t -> p h t"), KTEV[0][:])], [KTEV[1]], [kres[i]], KTEV[1])
            for fbv in (4, 5):
                wt = load_w("in", fbv)
                for t4 in range(4):
                    ps = rot([0, 1, 2, 3])
                    for dc in range(DC):
                        mm(ps, UA[0][:, dc, t4 * 128:(t4 + 1) * 128], wt[0][:, dc, :], dc == 0, dc == DC - 1, [wt[1], UA[1]])
                    cp("dve", VEV[0][:, t4, (fbv - 4) * 512:(fbv - 3) * 512], ps[0][:], [ps[1]], [VEV[1]])
            S.dma([(vscr[:, i].rearrange("h p t e -> p t h e"), VEV[0][:].rearrange("p t (h e) -> p t h e", h=8))],
                  [VEV[1]], [vres[i]], VEV[1])
            for fb in (6, 7):
                proj_fm("in", fb, UA, DC, lambda fc, ps, ci=ci: act(SIN[ci][0][:, fc - 24, :], ps[0][:], AF.Copy, [ps[1]], [SIN[ci][1]]))
        def ssm_ab(q, ci):
            o = q // 4; j = q * 2 + ci
            tb, bt, ct = TABS[q % 2]
            if ci == 0:
                S.dma([(tb[0][:], tab_s[q]), (bt[0][:], bt_s[q]), (ct[0][:], ct_s[q])], [tabres[q], btres[q], ctres[q]],
                      [tb[1], bt[1], ct[1]], tb[1])
            tres = [tb[1], bt[1], ct[1]]
            Zre, Zim = PS[4], PS[5]
            mm(Zre, bt[0][:, 0, :], SIN[ci][0][:, o, :], True, True, tres + [SIN[ci][1]])
            mm(Zim, bt[0][:, 1, :], SIN[ci][0][:, o, :], True, True, tres + [SIN[ci][1]])
            cosv, sinv = tb[0][:, 0, 1:513], tb[0][:, 1, 1:513]
            t1, t2, t3, t4 = T4
            tt("dve", t1[0][:], Zre[0][:], cosv, ALU.mult, [Zre[1]] + tres, [t1[1]])
            tt("dve", t2[0][:], Zim[0][:], sinv, ALU.mult, [Zim[1]] + tres, [t2[1]])
            tt("dve", t3[0][:], Zim[0][:], cosv, ALU.mult, [Zim[1]] + tres, [t3[1]])
            tt("dve", t4[0][:], Zre[0][:], sinv, ALU.mult, [Zre[1]] + tres, [t4[1]])
            tt("dve", t1[0][:], t1[0][:], t2[0][:], ALU.add, [t1[1], t2[1]], [t1[1]])
            tt("pool", t3[0][:], t3[0][:], t4[0][:], ALU.subtract, [t3[1], t4[1]], [t3[1]])
            wre, wim = WW[j % 3]
            rb = RA[0][:, q:q + 1].to_broadcast([128, 512])
            S.op("dve", lambda h: h.tensor_tensor_scan(out=wre[0][:, 1:513], data0=rb, data1=t1[0][:],
                 initial=CAR[0][:, q, 0:1], op0=ALU.mult, op1=ALU.add), [RA[1], t1[1], CAR[1]], [wre[1]])
            S.op("dve", lambda h: h.tensor_tensor_scan(out=wim[0][:, 1:513], data0=rb, data1=t3[0][:],
                 initial=CAR[0][:, q, 1:2], op0=ALU.mult, op1=ALU.add), [RA[1], t3[1], CAR[1]], [wim[1]])
            ts("dve", CTMP[0][:, 0:1], wre[0][:, 512:513], CS512[0][:, q, 0:1], ALU.mult, [wre[1], CS512[1]], [CTMP[1]])
            ts("dve", CTMP[0][:, 1:2], wre[0][:, 512:513], CS512[0][:, q, 1:2], ALU.mult, [wre[1], CS512[1]], [CTMP[1]])
            stt(CAR[0][:, q, 0:1], wim[0][:, 512:513], CS512[0][:, q, 2:3], CTMP[0][:, 0:1], ALU.mult, ALU.add,
                [wim[1], CS512[1], CTMP[1]], [CAR[1]])
            stt(CAR[0][:, q, 1:2], wim[0][:, 512:513], CS512[0][:, q, 0:1], CTMP[0][:, 1:2], ALU.mult, ALU.add,
                [wim[1], CS512[1], CTMP[1]], [CAR[1]])
            x1, x2, x3, x4 = XX[j % 3]
            tt("pool", x1[0][:], wre[0][:, 1:513], cosv, ALU.mult, [wre[1]] + tres, [x1[1]])
            tt("pool", x2[0][:], wim[0][:, 1:513], sinv, ALU.mult, [wim[1]] + tres, [x2[1]])
            tt("pool", x3[0][:], wre[0][:, 1:513], sinv, ALU.mult, [wre[1]] + tres, [x3[1]])
            tt("pool", x4[0][:], wim[0][:, 1:513], cosv, ALU.mult, [wim[1]] + tres, [x4[1]])

        def ssm_c(q, ci):
            o = q // 4; j = q * 2 + ci
            tb, bt, ct = TABS[q % 2]
            tres = [tb[1], bt[1], ct[1]]
            x1, x2, x3, x4 = XX[j % 3]
            Y = PS[6 + ci]
            first = (q % 4 == 0); last = (q % 4 == 3)
            mm(Y, ct[0][:, 0, :], x1[0][:], first, False, tres + [x1[1]])
            mm(Y, ct[0][:, 1, :], x2[0][:], False, False, tres + [x2[1]])
            mm(Y, ct[0][:, 2, :], x3[0][:], False, False, tres + [x3[1]])
            mm(Y, ct[0][:, 2, :], x4[0][:], False, last, tres + [x4[1]])
            if last:
                stt(EPI1[0][:], SIN[ci][0][:, o, :], DVEC[0][:, o:o + 1], Y[0][:], ALU.mult, ALU.add,
                    [SIN[ci][1], DVEC[1], Y[1]], [EPI1[1]])
                if ci == 0:
                    ts("dve", YSO[0][:, o, :], EPI1[0][:], MSEL[0][:, 0:1], ALU.mult, [EPI1[1], MSEL[1]], [YSO[1]])
                else:
                    stt(YSO[0][:, o, :], EPI1[0][:], MSEL[0][:, 1:2], YSO[0][:, o, :], ALU.mult, ALU.add,
                        [EPI1[1], MSEL[1], YSO[1]], [YSO[1]])
        ssm_i = [0]

        def ssm_step():
            j = ssm_i[0]
            if j < 64:
                ssm_ab(j // 2, j % 2)
            if 0 <= j - 2 < 64:
                ssm_c((j - 2) // 2, (j - 2) % 2)
            ssm_i[0] += 1

        c0 = s * NT
        norm_x(xT_own, c0, XR2, SQ2E, RSTD2E, 0, U, PS[0])
        S.op("pool", lambda h: h.memset(QP[0][64:128, :, 0, :], 0.0), [], [QP[1]])
        S.op("pool", lambda h: h.memset(QP[0][0:64, :, 1, :], 0.0), [], [QP[1]])

        def evq(fc, ps):
            act(QP[0][0:64, fc, 0, :], ps[0][0:64, :], AF.Copy, [ps[1]], [QP[1]])
            act(QP[0][64:128, fc, 1, :], ps[0][64:128, :], AF.Copy, [ps[1]], [QP[1]])
        if s == 0:
            late_upto(4, True)
        for fb in (0, 1):
            proj_fm("in", fb, U, DC, evq)
        NK = 2 * s + 2
        kvctr = [0]; ptctr = [0]
        total_tiles = 8 * 2 * NK * 4; tile_ctr = [0]
        Ob, Zb = PS[2], PS[3]
        for hd in range(8):
            tiles = [(c, kc, kt) for c in range(2) for kc in range(NK) for kt in range(4)]
            loaded = {}
            sbank = {}

            def emit_qk(i):
                c, kc, kt = tiles[i]
                if (c, kc) not in loaded:
                    n_ = kvctr[0]; kvctr[0] += 1
                    kw_, vw_ = KTW[n_ % 4], VTW[n_ % 4]
                    S.dma([(kw_[0][:], kscr[hd, :, kc * NT:(kc + 1) * NT])], [kres[kc]], [kw_[1]], kw_[1])
                    S.dma([(vw_[0][:], vscr[hd, kc])], [vres[kc]], [vw_[1]], vw_[1])
                    loaded[(c, kc)] = (kw_, vw_)
                kw_, vw_ = loaded[(c, kc)]
                sp_ = rot([0, 1])
                diag = kc >= NK - 2
                mm(sp_, kw_[0][:, kt * 128:(kt + 1) * 128], QP[0][:, hd, c, :], True, not diag, [kw_[1], QP[1]])
                if diag:
                    mm(sp_, IDENT[0][:], MASK[0][:, kc - (NK - 2), kt, :], False, True, [IDENT[1], MASK[1]])
                sbank[i] = sp_

            emit_qk(0)
            for i in range(len(tiles)):
                if i + 1 < len(tiles):
                    emit_qk(i + 1)
                c, kc, kt = tiles[i]
                kw_, vw_ = loaded[(c, kc)]
                sp_ = sbank.pop(i)
                pt = PT[ptctr[0] % 4]; ptctr[0] += 1
                act(pt[0][:], sp_[0][:], AF.Exp, [sp_[1]], [pt[1]], scale=0.125)
                fst = (kc == 0 and kt == 0); lst = (kc == NK - 1 and kt == 3)
                mm(Ob, vw_[0][:, kt, :], pt[0][:], fst, lst, [vw_[1], pt[1]])
                mm(Zb, ONES[0][:], pt[0][:], fst, lst, [ONES[1], pt[1]])
                if lst:
                    act(OE[2 * c][0][:], Ob[0][:], AF.Copy, [Ob[1]], [OE[2 * c][1]])
                    act(OE[2 * c + 1][0][:], Zb[0][:], AF.Copy, [Zb[1]], [OE[2 * c + 1][1]])
                tile_ctr[0] += 1
                while ssm_i[0] < (tile_ctr[0] * 66) // total_tiles:
                    ssm_step()
                if s == 0:
                    late_upto(4 + (tile_ctr[0] * (len(late) - 3)) // total_tiles, False)
            ea, eb, ec = EP
            act(OE[1][0][:], OE[1][0][:], AF.Ln, [OE[1][1]], [OE[1][1]])
            act(OE[1][0][:], OE[1][0][:], AF.Exp, [OE[1][1]], [OE[1][1]], scale=-1.0)
            tt("dve", eb[0][:], OE[0][0][:], OE[1][0][:], ALU.mult, [OE[0][1], OE[1][1]], [eb[1]])
            act(OE[3][0][:], OE[3][0][:], AF.Ln, [OE[3][1]], [OE[3][1]])
            act(OE[3][0][:], OE[3][0][:], AF.Exp, [OE[3][1]], [OE[3][1]], scale=-1.0)
            tt("dve", ec[0][:], OE[2][0][:], OE[3][0][:], ALU.mult, [OE[2][1], OE[3][1]], [ec[1]])
            stt(eb[0][:], ec[0][:], NEGLAM[0][:, 0:1], eb[0][:], ALU.mult, ALU.add, [ec[1], NEGLAM[1], eb[1]], [eb[1]])
            tt("dve", SQB[0][:], eb[0][:], eb[0][:], ALU.mult, [eb[1]], [SQB[1]])
            msb = rot([0, 1])
            mm(msb, ONES[0][:], SQB[0][:], True, True, [ONES[1], SQB[1]])
            act(ea[0][:], msb[0][:], AF.Ln, [msb[1]], [ea[1]], scale=1.0 / 128, bias=EPS)
            act(ea[0][:], ea[0][:], AF.Exp, [ea[1]], [ea[1]], scale=-0.5)
            stt(YA[0][:, hd, :], eb[0][:], SUBW[0][:, 0:1], ea[0][:], ALU.mult, ALU.mult, [eb[1], SUBW[1], ea[1]], [YA[1]])
        while ssm_i[0] < 66:
            ssm_step()
        if s == 0:
            late_upto(len(late), True)
        for o in range(8):
            act(YSO[0][:, o, :], YSO[0][:, o, :], AF.Gelu, [YSO[1]], [YSO[1]])
            cp("pool", GB[0][:, o, :], YSO[0][:, o, :], [YSO[1]], [GB[1]])
        for fb in range(2):
            wt = load_w("glu", fb)
            for j in range(4):
                fc = fb * 4 + j
                ps = rot([0, 1, 2, 3])
                for dc in range(8):
                    mm(ps, wt[0][:, dc, j * 128:(j + 1) * 128], GB[0][:, dc, :], dc == 0, dc == 7, [wt[1], GB[1]])
                ft = FT[fc % 2]
                act(ft[0][:], ps[0][:], AF.Sigmoid, [ps[1], BGLU[1]], [ft[1]], bias=BGLU[0][:, fc:fc + 1])
                tt("dve", YSB[0][:, fc, :], YSO[0][:, fc, :], ft[0][:], ALU.mult, [YSO[1], ft[1]], [YSB[1]])
        for fb in range(4):
            wt = load_w("in", 8 + fb)
            for j in range(4):
                ps = rot([0, 1, 2, 3])
                for dc in range(DC):
                    mm(ps, wt[0][:, dc, j * 128:(j + 1) * 128], U[0][:, dc, :], dc == 0, dc == DC - 1, [wt[1], U[1]])
                act(SA[0][:, j, :], ps[0][:], AF.Sigmoid, [ps[1]], [SA[1]])
            wt = load_w("a", fb)
            for j in range(4):
                ps = rot([0, 1, 2, 3])
                for dc in range(8):
                    mm(ps, wt[0][:, dc, j * 128:(j + 1) * 128], YA[0][:, dc, :], dc == 0, dc == 7, [wt[1], YA[1]])
                tt("dve", MT[0][:, j, :], ps[0][:], SA[0][:, j, :], ALU.mult, [ps[1], SA[1]], [MT[1]])
            wt = load_w("in", 12 + fb)
            for j in range(4):
                ps = rot([0, 1, 2, 3])
                for dc in range(DC):
                    mm(ps, wt[0][:, dc, j * 128:(j + 1) * 128], U[0][:, dc, :], dc == 0, dc == DC - 1, [wt[1], U[1]])
                act(SA[0][:, j, :], ps[0][:], AF.Sigmoid, [ps[1]], [SA[1]])
            wt = load_w("s", fb)
            for j in range(4):
                ps = rot([0, 1, 2, 3])
                for dc in range(8):
                    mm(ps, wt[0][:, dc, j * 128:(j + 1) * 128], YSB[0][:, dc, :], dc == 0, dc == 7, [wt[1], YSB[1]])
                tt("dve", SA[0][:, j, :], ps[0][:], SA[0][:, j, :], ALU.mult, [ps[1], SA[1]], [SA[1]])
                tt("dve", M[0][:, fb * 4 + j, :], SA[0][:, j, :], MT[0][:, j, :], ALU.add, [SA[1], MT[1]], [M[1]])
        stat = PS[7]
        for fb in range(4):
            wt = load_w("o", fb)
            for j in range(4):
                fc = fb * 4 + j
                ps = rot([0, 1, 2, 3])
                for dc in range(DC):
                    mm(ps, wt[0][:, dc, j * 128:(j + 1) * 128], M[0][:, dc, :], dc == 0, dc == DC - 1, [wt[1], M[1]])
                act(BIG1[0][:, fc, :], ps[0][:], AF.Copy, [ps[1]], [BIG1[1]])
                sq = SQ2[fc % 2]
                act(sq[0][:], ps[0][:], AF.Square, [ps[1]], [sq[1]])
                mm(stat, ONES[0][:], sq[0][:], fc == 0, fc == 15, [ONES[1], sq[1]])
        act(RSTD2[0][:], stat[0][:], AF.Sqrt, [stat[1]], [RSTD2[1]], scale=1.0 / D, bias=EPS)
        recip(RSTD2[0][:], RSTD2[0][:], [RSTD2[1]], [RSTD2[1]])
        S.dma([(H[0][:], xT_own[:, c0:c0 + NT].rearrange("(c p) t -> p c t", p=128))], [], [H[1]], H[1])
        for fc in range(DC):
            stt(BIG1[0][:, fc, :], BIG1[0][:, fc, :], VEC16[0][:, 1, fc:fc + 1], RSTD2[0][:], ALU.mult, ALU.mult,
                [BIG1[1], VEC16[1], RSTD2[1]], [BIG1[1]])
            tt("pool", H[0][:, fc, :], H[0][:, fc, :], BIG1[0][:, fc, :], ALU.add, [H[1], BIG1[1]], [H[1]])
        stat = PS[6]
        for fc in range(DC):
            sq = SQ2[fc % 2]
            act(sq[0][:], H[0][:, fc, :], AF.Square, [H[1]], [sq[1]])
            mm(stat, ONES[0][:], sq[0][:], fc == 0, fc == 15, [ONES[1], sq[1]])
        act(RSTD2[0][:], stat[0][:], AF.Sqrt, [stat[1]], [RSTD2[1]], scale=1.0 / D, bias=EPS)
        recip(RSTD2[0][:], RSTD2[0][:], [RSTD2[1]], [RSTD2[1]])
        for fc in range(DC):
            stt(U[0][:, fc, :], H[0][:, fc, :], VEC16[0][:, 2, fc:fc + 1], RSTD2[0][:], ALU.mult, ALU.mult,
                [H[1], VEC16[1], RSTD2[1]], [U[1]])
        stat = PS[7]
        for half, fbs in enumerate((range(0, 6), range(6, 11))):
            nfc = len(fbs) * 4
            for fb in fbs:
                wg = load_w("g", fb); wu = load_w("u", fb)
                for j in range(4):
                    pg = rot([0, 1]); pu = rot([2, 3])
                    for dc in range(DC):
                        mm(pg, wg[0][:, dc, j * 128:(j + 1) * 128], U[0][:, dc, :], dc == 0, dc == DC - 1, [wg[1], U[1]])
                    for dc in range(DC):
                        mm(pu, wu[0][:, dc, j * 128:(j + 1) * 128], U[0][:, dc, :], dc == 0, dc == DC - 1, [wu[1], U[1]])
                    ft = FT[j % 2]
                    act(ft[0][:], pg[0][:], AF.Silu, [pg[1]], [ft[1]])
                    tt("dve", A[0][:, (fb - fbs[0]) * 4 + j, :], pu[0][:], ft[0][:], ALU.mult, [pu[1], ft[1]], [A[1]])
            dc0 = fbs[0] * 4
            npc = nfc // 2
            for fbo in range(4):
                banks = [PS[2], PS[3], PS[4], PS[5]]
                for pc in range(2):
                    wt = load_w("d", fbo, dc0 + pc * npc, npc)
                    for j in range(4):
                        for dl in range(npc):
                            mm(banks[j], wt[0][:, dl, j * 128:(j + 1) * 128], A[0][:, pc * npc + dl, :],
                               pc == 0 and dl == 0, pc == 1 and dl == npc - 1, [wt[1], A[1]])
                for j in range(4):
                    fc = fbo * 4 + j
                    if half == 0:
                        act(BIG1[0][:, fc, :], banks[j][0][:], AF.Copy, [banks[j][1]], [BIG1[1]])
                    else:
                        tt("dve", BIG1[0][:, fc, :], banks[j][0][:], BIG1[0][:, fc, :], ALU.add, [banks[j][1], BIG1[1]], [BIG1[1]])
                        sq = SQ2[fc % 2]
                        act(sq[0][:], BIG1[0][:, fc, :], AF.Square, [BIG1[1]], [sq[1]])
                        mm(stat, ONES[0][:], sq[0][:], fc == 0, fc == 15, [ONES[1], sq[1]])
        act(RSTD2[0][:], stat[0][:], AF.Sqrt, [stat[1]], [RSTD2[1]], scale=1.0 / D, bias=EPS)
        recip(RSTD2[0][:], RSTD2[0][:], [RSTD2[1]], [RSTD2[1]])
        for fc in range(DC):
            stt(BIG1[0][:, fc, :], BIG1[0][:, fc, :], VEC16[0][:, 3, fc:fc + 1], RSTD2[0][:], ALU.mult, ALU.mult,
                [BIG1[1], VEC16[1], RSTD2[1]], [BIG1[1]])
            tt("pool", BIG1[0][:, fc, :], BIG1[0][:, fc, :], H[0][:, fc, :], ALU.add, [H[1], BIG1[1]], [BIG1[1]])
        S.dma([(outT[:, c0:c0 + NT].rearrange("(c p) t -> p c t", p=128), BIG1[0][:])], [BIG1[1]], [], BIG1[1])
    E = S.E["sp"]
    E.prog.append(([(BIG1[1].dsem, BIG1[1].dcnt)], None, None))

    with nc.allow_non_contiguous_dma(reason="param layouts"), nc.Block() as block:
        @block.tensor
        def _(h): S.replay(h, "pe")

        @block.scalar
        def _(h): S.replay(h, "act")

        @block.vector
        def _(h): S.replay(h, "dve")

        @block.gpsimd
        def _(h): S.replay(h, "pool")

        @block.sync
        def _(h): S.replay(h, "sp")
    stack.close()
    return nc


def host_inputs(NCH, x, w_in, lambda_q1, lambda_k1, lambda_q2, lambda_k2, subln_w, ssm_a_re, ssm_a_im,
                ssm_log_dt, ssm_b_re, ssm_b_im, ssm_c_re, ssm_c_im, ssm_d, w_glu, b_glu, w_attn_branch,
                w_ssm_branch, w_out, norm_mix_pre, norm_mix_post, w_ffn_gate, w_ffn_up, w_ffn_down,
                norm_ffn_pre, norm_ffn_post):
    f32 = np.float32
    bf = ml_dtypes.bfloat16
    c = lambda a: np.ascontiguousarray(np.asarray(a, dtype=f32))
    vec = lambda v: c(np.asarray(v)[0].reshape(-1, 128).T)
    shared = {
        "w_in": c(w_in[0]), "w_glu": c(w_glu[0]), "w_a": c(w_attn_branch[0]), "w_s": c(w_ssm_branch[0]),
        "w_o": c(w_out[0]), "w_g": c(w_ffn_gate[0]), "w_u": c(w_ffn_up[0]), "w_d": c(w_ffn_down[0]),
        "vec16": c(np.stack([vec(norm_mix_pre), vec(norm_mix_post), vec(norm_ffn_pre), vec(norm_ffn_post)], axis=1)),
        "bglu": vec(b_glu), "dvec": c(np.asarray(ssm_d)[0].reshape(8, 128).T),
        "lamv": c(np.broadcast_to(np.stack([np.asarray(v)[0] for v in (lambda_q1, lambda_k1, lambda_q2, lambda_k2)])[None], (128, 4, 64))),
        "subw": c(np.asarray(subln_w)[0].reshape(128, 1)),
        "ident": np.eye(128, dtype=f32).astype(bf),
        "iota": c(np.broadcast_to(np.arange(513, dtype=f32)[None], (128, 513))),
    }
    are, aim, ldt = np.asarray(ssm_a_re)[0], np.asarray(ssm_a_im)[0], np.asarray(ssm_log_dt)[0]
    A_ = lambda m: m.reshape(32, 128).T
    shared["ssmA"] = c(np.stack([A_(are), A_(aim), A_(np.repeat(ldt[:, None], 64, 1))], axis=1))
    bre, bim = np.asarray(ssm_b_re)[0], np.asarray(ssm_b_im)[0]
    cre, cim = np.asarray(ssm_c_re)[0], np.asarray(ssm_c_im)[0]
    P5 = np.zeros((5, 128, 32, 128), f32); C2 = np.zeros((2, 128, 32, 128), f32)
    for q in range(32):
        o = q // 4
        for gl in range(8):
            g = 8 * o + gl
            rows = slice(gl * 16, gl * 16 + 16)
            for two in range(2):
                cols = slice(two * 64, two * 64 + 64)
                P5[0, rows, q, cols] = are[g][None, :]
                P5[1, rows, q, cols] = aim[g][None, :]
                P5[2, rows, q, cols] = ldt[g]
                if gl == 2 * (q % 4) + two:
                    P5[3, rows, q, cols] = bre[g].T
                    P5[4, rows, q, cols] = bim[g].T
                    C2[0, cols, q, rows] = cre[g].T
                    C2[1, cols, q, rows] = cim[g].T
    shared["ssmP"] = P5; shared["ssmC"] = C2
    xs = np.asarray(x, dtype=f32)
    maps = []
    for core in range(8):
        b, h = core // 2, core % 2
        xt = np.ascontiguousarray(xs[b, :NCH * NT].T)
        own = np.concatenate([xt[:, (2 * s + h) * NT:(2 * s + h + 1) * NT] for s in range(NCH // 2)], axis=1)
        mask = np.zeros((128, 2, 4, 512), f32)
        ii = np.arange(128)[:, None]; jj = np.arange(512)[None, :]
        for r in range(2):
            for kt in range(4):
                allowed = ((r - h) * 512 + kt * 128 + ii) <= jj
                mask[:, r, kt, :] = np.where(allowed, 0.0, -30000.0)
        msel = np.zeros((128, 2), f32); msel[:, h] = 1.0
        m = dict(shared)
        m.update({"xT_all": xt, "xT_own": np.ascontiguousarray(own), "mask": mask.astype(bf), "msel": msel})
        maps.append(m)
    return maps


_NC_CACHE = {}


def run(NCH, **inputs):
    maps = host_inputs(NCH, **inputs)
    if NCH not in _NC_CACHE:
        _NC_CACHE[NCH] = build_nc(NCH)
    nc = _NC_CACHE[NCH]
    res = run_bass_kernel_spmd(nc, maps, core_ids=list(range(8)))
    out = np.zeros((4, NCH * NT, D), np.float32)
    for core in range(8):
        b, h = core // 2, core % 2
        oT = res.results[core]["outT"]
        for s in range(NCH // 2):
            out[b, (2 * s + h) * NT:(2 * s + h + 1) * NT, :] = oT[:, s * NT:(s + 1) * NT].T
    return out


def kernel(**inputs):
    return run(16, **inputs)
```
